# Optimizing a Trainium2 kernel written in Bass

```python
import math
import jax, jax.numpy as jnp
from jax import lax
import numpy as np

D_MODEL = 1024
BATCH = 8
SEQ = 4096
DEPTH = 1

ATTN_HEAD_DIM = 64
ATTN_WIDTH = D_MODEL // 2
ATTN_HEADS = ATTN_WIDTH // ATTN_HEAD_DIM
DILATED_PATTERNS = ((128, 1), (512, 4), (2048, 16))
ATTN_BLOCK = 128
HGRN_HEAD_DIM = 128
HGRN_WIDTH = D_MODEL - ATTN_WIDTH
HGRN_HEADS = HGRN_WIDTH // HGRN_HEAD_DIM
HGRN_CHUNK = 64
MIX_WIDTH = ATTN_WIDTH + HGRN_WIDTH
IN_PROJ_WIDTH = 3 * ATTN_WIDTH + 4 * HGRN_WIDTH
D_FF = 4 * D_MODEL
RMS_EPS = 1e-6

kernel_name = "hymba_dilated_attn_hgrn2_sqrelu"


def rmsnorm(x, gain):
    xf = x.astype(jnp.float32)
    y = xf * lax.rsqrt(jnp.mean(xf * xf, axis=-1, keepdims=True) + RMS_EPS)
    return (y * gain.astype(jnp.float32)).astype(x.dtype)


def alibi_slopes(n_heads):
    return jnp.exp2(-8.0 * jnp.arange(1, n_heads + 1, dtype=jnp.float32) / n_heads)


def dilated_branch(q, k, v, slopes, window, dilation):
    B, S, H, Dh = q.shape
    span = window // dilation
    L = S // dilation
    nb = -(-L // ATTN_BLOCK)
    Lp = nb * ATTN_BLOCK
    Bd = B * dilation

    def to_sub(t):
        t = t.reshape(B, L, dilation, H, Dh).transpose(0, 2, 1, 3, 4).reshape(Bd, L, H, Dh)
        t = jnp.pad(t, ((0, 0), (0, Lp - L), (0, 0), (0, 0)))
        return t.reshape(Bd, nb, ATTN_BLOCK, H, Dh)

    def with_prev(t):
        prev = jnp.pad(t[:, :-1], ((0, 0), (1, 0), (0, 0), (0, 0), (0, 0)))
        return jnp.concatenate([prev, t], axis=2)

    qb = to_sub(q)
    kk = with_prev(to_sub(k))
    vv = with_prev(to_sub(v))
    s = jnp.einsum('bnqhd,bnkhd->bnhqk', qb, kk).astype(jnp.float32) * (Dh ** -0.5)
    qi = jnp.arange(ATTN_BLOCK)[:, None] + ATTN_BLOCK
    kj = jnp.arange(2 * ATTN_BLOCK)[None, :]
    dist = qi - kj
    blk = jnp.arange(nb)[:, None, None]
    valid = (dist >= 0) & (dist <= span) & (blk * ATTN_BLOCK + kj - ATTN_BLOCK >= 0)
    bias = -slopes[:, None, None] * (dist * dilation).astype(jnp.float32)
    s = jnp.where(valid[None, :, None], s + bias[None, None], -jnp.inf)
    m = jnp.max(s, axis=-1, keepdims=True)
    p = jnp.exp(s - m)
    den = jnp.sum(p, axis=-1)
    o = jnp.einsum('bnhqk,bnkhd->bnqhd', p, vv.astype(jnp.float32))
    o = o / jnp.transpose(den, (0, 1, 3, 2))[..., None]
    lse = jnp.transpose(m[..., 0] + jnp.log(den), (0, 1, 3, 2))

    def from_sub(t):
        t = t.reshape((Bd, Lp) + t.shape[3:])[:, :L]
        t = t.reshape((B, dilation, L) + t.shape[2:])
        t = jnp.swapaxes(t, 1, 2)
        return t.reshape((B, S) + t.shape[3:])

    return from_sub(o), from_sub(lse)


def dilated_mixture_attention(q, k, v, slopes):
    outs, lses = [], []
    for window, dilation in DILATED_PATTERNS:
        o, lse = dilated_branch(q, k, v, slopes, window, dilation)
        outs.append(o)
        lses.append(lse)
    w = jax.nn.softmax(jnp.stack(lses, axis=0), axis=0)
    o = jnp.sum(w[..., None] * jnp.stack(outs, axis=0), axis=0)
    return o.astype(q.dtype)


def hgrn2_recurrence(q, f_pre, i, gate, lower_bound, out_gain):
    B, S, _ = q.shape
    H, D, C = HGRN_HEADS, HGRN_HEAD_DIM, HGRN_CHUNK
    shape = (B, S, H, D)
    lb = lower_bound.reshape(H, D)
    qf = jax.nn.silu(q.astype(jnp.float32)).reshape(shape)
    log_f = jnp.logaddexp(jnp.log(lb), jnp.log1p(-lb) + jax.nn.log_sigmoid(f_pre.astype(jnp.float32).reshape(shape)))
    kf = -jnp.expm1(log_f)
    vf = i.astype(jnp.float32).reshape(shape)
    nC = S // C

    def to_chunks(t):
        return t.reshape(B, nC, C, H, D).transpose(1, 0, 3, 2, 4)

    causal = jnp.tril(jnp.ones((C, C), dtype=bool))[:, :, None]

    def step(state, xs):
        qc, kc, vc, gc = xs
        b = jnp.cumsum(gc, axis=2)
        o_inter = jnp.einsum('bhtk,bhkv->bhtv', qc * jnp.exp(b), state)
        diff = b[:, :, :, None, :] - b[:, :, None, :, :]
        decay = jnp.exp(jnp.where(causal, diff, -jnp.inf))
        a = jnp.einsum('bhtk,bhsk,bhtsk->bhts', qc, kc, decay)
        o = o_inter + jnp.einsum('bhts,bhsv->bhtv', a, vc)
        b_last = b[:, :, -1:, :]
        new_state = jnp.exp(b_last[:, :, 0, :, None]) * state + jnp.einsum(
            'bhsk,bhsv->bhkv', kc * jnp.exp(b_last - b), vc)
        return new_state, o

    state0 = jnp.zeros((B, H, D, D), jnp.float32)
    _, o = lax.scan(step, state0, (to_chunks(qf), to_chunks(kf), to_chunks(vf), to_chunks(log_f)))
    o = o.transpose(1, 0, 3, 2, 4).reshape(shape)
    o = rmsnorm(o, out_gain.reshape(H, D)).reshape(B, S, H * D)
    return (o * jax.nn.silu(gate.astype(jnp.float32))).astype(q.dtype)


def setup_inputs(seed: int = 0) -> dict:
    key = jax.random.key(seed)
    ks = jax.random.split(key, 12)
    f32 = jnp.float32

    def gain(k, shape):
        return 1.0 + 0.02 * jax.random.normal(k, shape, f32)

    return {
        "x": jax.random.normal(ks[0], (BATCH, SEQ, D_MODEL), f32),
        "mix_pre_norm": gain(ks[1], (DEPTH, D_MODEL)),
        "w_in": jax.random.normal(ks[2], (DEPTH, D_MODEL, IN_PROJ_WIDTH), f32) * D_MODEL ** -0.5,
        "attn_out_norm": gain(ks[3], (DEPTH, ATTN_WIDTH)),
        "hgrn_lb_logits": 0.5 * jax.random.normal(ks[4], (DEPTH + 1, HGRN_WIDTH), f32),
        "hgrn_out_norm": gain(ks[5], (DEPTH, HGRN_WIDTH)),
        "w_out": jax.random.normal(ks[6], (DEPTH, MIX_WIDTH, D_MODEL), f32) * MIX_WIDTH ** -0.5,
        "mix_post_norm": gain(ks[7], (DEPTH, D_MODEL)),
        "mlp_pre_norm": gain(ks[8], (DEPTH, D_MODEL)),
        "w_ff1": jax.random.normal(ks[9], (DEPTH, D_MODEL, D_FF), f32) * D_MODEL ** -0.5,
        "w_ff2": jax.random.normal(ks[10], (DEPTH, D_FF, D_MODEL), f32) * D_FF ** -0.5,
        "mlp_post_norm": gain(ks[11], (DEPTH, D_MODEL)),
    }


def reference(x, mix_pre_norm, w_in, attn_out_norm, hgrn_lb_logits, hgrn_out_norm, w_out,
              mix_post_norm, mlp_pre_norm, w_ff1, w_ff2, mlp_post_norm):
    B, S, _ = x.shape
    slopes = alibi_slopes(ATTN_HEADS)
    lower_bounds = jnp.cumsum(jax.nn.softmax(hgrn_lb_logits.astype(jnp.float32), axis=0), axis=0)
    aw, hw = ATTN_WIDTH, HGRN_WIDTH
    splits = [aw, 2 * aw, 3 * aw, 3 * aw + hw, 3 * aw + 2 * hw, 3 * aw + 3 * hw]
    for layer in range(DEPTH):
        h = rmsnorm(x, mix_pre_norm[layer])
        proj = h @ w_in[layer]
        q_a, k_a, v_a, q_h, f_h, i_h, g_h = jnp.split(proj, splits, axis=-1)
        ahead = (B, S, ATTN_HEADS, ATTN_HEAD_DIM)
        attn = dilated_mixture_attention(q_a.reshape(ahead), k_a.reshape(ahead), v_a.reshape(ahead), slopes)
        attn = rmsnorm(attn.reshape(B, S, ATTN_WIDTH), attn_out_norm[layer])
        rec = hgrn2_recurrence(q_h, f_h, i_h, g_h, lower_bounds[layer], hgrn_out_norm[layer])
        mixed = jnp.concatenate([attn, rec], axis=-1) @ w_out[layer]
        x = x + rmsnorm(mixed, mix_post_norm[layer])
        h = rmsnorm(x, mlp_pre_norm[layer])
        ff = jnp.square(jax.nn.relu(h @ w_ff1[layer])) @ w_ff2[layer]
        x = x + rmsnorm(ff, mlp_post_norm[layer])
    return x
```

```python
import contextlib
import os
import numpy as np
import ml_dtypes
import concourse.bass as bass
import concourse.mybir as mybir
from concourse.bass_utils import run_bass_kernel_spmd

F32 = mybir.dt.float32
BF16 = mybir.dt.bfloat16
AF = mybir.ActivationFunctionType
ALU = mybir.AluOpType

S = 4096
D = 1024
NCORE = 8
DFF = 4096
INW = 3584
EPS = 1e-6
PATTERNS = (1, 4, 16)
ARENA = 188 * 1024


class Res:
    __slots__ = ("name", "w", "r")

    def __init__(self, name):
        self.name = name
        self.w = None
        self.r = []


class Prog:
    ENG = ("pe", "act", "dve", "pool", "sp")

    def __init__(self, nc, stack, n_dma_slots=8):
        self.nc = nc
        self.stack = stack
        self.ops = {e: [] for e in self.ENG}
        self.sems = {}
        self.count = {}
        for e in self.ENG:
            self.sems[e] = stack.enter_context(nc.semaphore("s_" + e))
            self.count[e] = 0
        self.seen = {e: {} for e in self.ENG}
        self.dma_slots = {}
        self.n_dma_slots = n_dma_slots
        self.nres = 0
        self.muted = False

    def res(self, name=None):
        self.nres += 1
        return Res(name or f"r{self.nres}")

    def sbuf(self, name, shape, dtype):
        return self.stack.enter_context(self.nc.sbuf_tensor("sb_" + name, list(shape), dtype))

    def psum(self, name, shape, dtype=F32):
        return self.stack.enter_context(self.nc.psum_tensor(name, list(shape), dtype))

    def _collect(self, reads, writes):
        waits = {}

        def add(ev):
            if ev is None:
                return
            k, v = ev
            if waits.get(k, 0) < v:
                waits[k] = v
        for r in reads:
            add(r.w)
        for w in writes:
            add(w.w)
            for ev in w.r:
                add(ev)
        return waits

    def _emit_waits(self, eng, waits):
        seen = self.seen[eng]
        out = []
        for k, v in waits.items():
            if k == eng and eng in ("pe", "sp"):
                continue
            if seen.get(k, 0) < v:
                seen[k] = v
                out.append((self.sems[k], v))
        return out

    def op(self, eng, fn, reads=(), writes=()):
        if self.muted:
            return None
        wl = self._emit_waits(eng, self._collect(reads, writes))
        self.count[eng] += 1
        ev = (eng, self.count[eng])
        sem = self.sems[eng]

        def run(e, fn=fn, wl=wl, sem=sem):
            for s, v in wl:
                e.wait_ge(s, v)
            fn(e).then_inc(sem, 1)
        self.ops[eng].append(run)
        for r in reads:
            r.r.append(ev)
        for w in writes:
            w.w = ev
            w.r = []
        return ev

    def dma(self, queue, out, in_, reads=(), writes=()):
        if self.muted:
            return None
        if queue not in self.dma_slots:
            slots = []
            for i in range(self.n_dma_slots):
                name = f"d_{queue}{i}"
                self.sems[name] = self.stack.enter_context(self.nc.semaphore(name))
                slots.append([name, 0])
            self.dma_slots[queue] = [slots, 0]
        slots, idx = self.dma_slots[queue]
        slot = slots[idx % len(slots)]
        self.dma_slots[queue][1] = idx + 1
        waits = self._collect(reads, writes)
        if slot[1] > 0 and waits.get(slot[0], 0) < slot[1]:
            waits[slot[0]] = slot[1]
        wl = self._emit_waits(queue, waits)
        slot[1] += 16
        ev = (slot[0], slot[1])
        sem = self.sems[slot[0]]

        def run(e, wl=wl, sem=sem, out=out, in_=in_):
            for s, v in wl:
                e.wait_ge(s, v)
            e.dma_start(out=out, in_=in_).then_inc(sem, 16)
        self.ops[queue].append(run)
        for r in reads:
            r.r.append(ev)
        for w in writes:
            w.w = ev
            w.r = []
        return ev

    def fence(self, engines=None, skip_dma_queues=()):
        targets = {e: self.count[e] for e in ("pe", "act", "dve", "pool") if self.count[e] > 0}
        for q, (slots, idx) in self.dma_slots.items():
            if q in skip_dma_queues:
                continue
            for name, val in slots:
                if val > 0:
                    targets[name] = val
        for eng in (engines or self.ENG):
            wl = self._emit_waits(eng, dict(targets))
            if not wl:
                continue

            def run(e, wl=wl):
                for s, v in wl:
                    e.wait_ge(s, v)
            self.ops[eng].append(run)

    def build(self):
        with self.nc.Block() as block:
            @block.tensor
            def _(e):
                for f in self.ops["pe"]:
                    f(e)

            @block.scalar
            def _(e):
                for f in self.ops["act"]:
                    f(e)

            @block.vector
            def _(e):
                for f in self.ops["dve"]:
                    f(e)

            @block.gpsimd
            def _(e):
                for f in self.ops["pool"]:
                    f(e)

            @block.sync
            def _(e):
                for f in self.ops["sp"]:
                    f(e)


def _sl(base, n, d):
    return slice(base, base + (n - 1) * d + 1, d)


class Arena:
    def __init__(self, ap_bf16, nbytes):
        self.ap = ap_bf16
        self.n = nbytes
        self.off = 0

    def reset(self, off=0):
        self.off = off

    def alloc(self, free_shape, dtype):
        esz = 4 if dtype == F32 else 2
        n = int(np.prod(free_shape)) * esz
        self.off = (self.off + 63) // 64 * 64
        assert self.off + n <= self.n, f"arena overflow {self.off + n} > {self.n}"
        v = self.ap[:, self.off // 2:(self.off + n) // 2]
        if dtype == F32:
            v = v.bitcast(F32)
        self.off += n
        if len(free_shape) == 2:
            v = v.rearrange("p (a b) -> p a b", b=free_shape[1])
        elif len(free_shape) == 3:
            v = v.rearrange("p (a b c) -> p a b c", b=free_shape[1], c=free_shape[2])
        return v


def _const_tables():
    ident = np.eye(128, dtype=np.float32).astype(ml_dtypes.bfloat16)
    k = np.arange(128)[:, None].astype(np.float64)
    q = np.arange(128)[None, :].astype(np.float64)
    mask = np.zeros((128, 24, 256), dtype=np.float32)
    for p, d in enumerate(PATTERNS):
        for h in range(8):
            slope = 2.0 ** (-(h + 1))
            dist_cur = q - k
            cur = np.where(dist_cur >= 0, np.exp(-slope * d * np.maximum(dist_cur, 0)), 0.0)
            dist_prev = q + 128 - k
            prv = np.where(dist_prev <= 128, np.exp(-slope * d * dist_prev), 0.0)
            mask[:, p * 8 + h, 0:128] = cur
            mask[:, p * 8 + h, 128:256] = prv
    mask = mask.astype(ml_dtypes.bfloat16)
    s_ = np.arange(128)[:, None]
    t_ = np.arange(128)[None, :]
    mbd = ((s_ // 64 == t_ // 64) & (s_ <= t_)).astype(np.float32).astype(ml_dtypes.bfloat16)
    return ident, mask, mbd


def build_program(dump=None, stop_after="E"):
    nc = bass.Bass("TRN2", target_bir_lowering=False)

    def din(name, shape, dt=F32):
        return nc.dram_tensor(name, list(shape), dt, kind="ExternalInput").ap()

    x_d = din("x", [S, D])
    w_in_d = din("w_in", [D, INW])
    w_out_d = din("w_out", [D, D])
    w1_d = din("w_ff1", [D, DFF])
    w2_d = din("w_ff2", [DFF, D])
    g1_d = din("g1", [128, D])
    gpost_d = din("gpost", [128, D])
    g2_d = din("g2", [128, D])
    gpost2_d = din("gpost2", [128, D])
    ga_d = din("ga", [128, 4])
    gh_d = din("gh", [128, 4])
    lbl_d = din("lbl", [128, 8])
    ident_d = din("c_ident", [128, 128], BF16)
    mask_d = din("c_mask", [128, 24, 256], BF16)
    mbd_d = din("c_mbd", [128, 128], BF16)
    out_d = nc.dram_tensor("out", [S, D], F32, kind="ExternalOutput").ap()
    x1s_d = nc.dram_tensor("x1s", [S, D], F32).ap()
    dumps = {}
    if dump:
        for name, shape, dt in dump:
            dumps[name] = nc.dram_tensor("dbg_" + name, list(shape), dt, kind="ExternalOutput").ap()

    with contextlib.ExitStack() as st:
        P = Prog(nc, st)
        ident = P.sbuf("ident", [128, 128], BF16)
        pshare = P.sbuf("pshare", [128, 24 * 256], BF16)
        mask = pshare[:].rearrange("p (a b) -> p a b", b=256)
        mbd = P.sbuf("mbd", [128, 128], BF16)
        ones_bf = P.sbuf("ones_bf", [128, 128], BF16)
        ga = P.sbuf("ga", [128, 4], F32)
        gh = P.sbuf("gh", [128, 4], F32)
        lbl = P.sbuf("lbl", [128, 8], F32)
        lbv = P.sbuf("lbv", [128, 16], F32)
        stat = P.sbuf("stat", [128, 16], F32)
        Sst = P.sbuf("Sst", [128, 4, 2, 128], F32)
        Sbf = pshare[:, 0:4 * 9 * 128].rearrange("p (a b c) -> p a b c", b=9, c=128)
        arena_t = P.sbuf("arena", [128, ARENA // 2], BF16)
        A = Arena(arena_t[:], ARENA)
        banks = [P.psum(f"bank{i}", [128, 512], F32) for i in range(8)]
        r_bank = [P.res(f"bank{i}") for i in range(8)]
        r_const = P.res("const")
        fin_res = []

        P.dma("sp", ident[:], ident_d, writes=[r_const])
        P.dma("sp", mask, mask_d, writes=[r_const])
        P.dma("sp", mbd[:], mbd_d, writes=[r_const])
        P.dma("sp", ga[:], ga_d, writes=[r_const])
        P.dma("sp", gh[:], gh_d, writes=[r_const])
        P.dma("sp", lbl[:], lbl_d, writes=[r_const])
        P.op("pool", lambda e: e.memset(ones_bf[:], 1.0), writes=[r_const])
        P.fence()

        def bank_bf(i):
            return banks[i][:].bitcast(BF16)

        def norm_tile(src_ap, gbc, xin, r_xin, xs, r_xs, sb, junk, r_g=None, src_reads=()):
            st0 = stat[:, 4 * sb:4 * sb + 1]
            st1 = stat[:, 4 * sb + 1:4 * sb + 2]
            st2 = stat[:, 4 * sb + 2:4 * sb + 3]
            r_st = r_stat[sb]
            P.dma("sp", xin, src_ap, reads=list(src_reads), writes=[r_xin])
            P.op("act", lambda e: e.activation(out=junk, in_=xin, func=AF.Square, accum_out=st0),
                 reads=[r_xin], writes=[r_st])
            P.op("act", lambda e: e.activation(out=st1, in_=st0, func=AF.Sqrt, bias=EPS, scale=1.0 / D),
                 reads=[r_st], writes=[r_st])
            P.op("dve", lambda e: e.reciprocal(out=st2, in_=st1), reads=[r_st], writes=[r_st])
            P.op("dve", lambda e: e.scalar_tensor_tensor(out=xs, in0=xin, scalar=st2, in1=gbc,
                                                         op0=ALU.mult, op1=ALU.mult),
                 reads=[r_xin, r_st] + ([r_g] if r_g is not None else []), writes=[r_xs])

        r_stat = [P.res(f"stat{i}") for i in range(4)]

        def transpose_tile(xs, r_xs, bk, dst, r_dst, evac_eng):
            pb = bank_bf(bk).rearrange("p (a b) -> p a b", b=128)

            def tr(e):
                ins = None
                for kc in range(8):
                    ins = e.transpose(pb[:, kc, :], in_=xs[:, kc * 128:(kc + 1) * 128], identity=ident[:])
                return ins
            P.op("pe", tr, reads=[r_xs], writes=[r_bank[bk]])
            if evac_eng == "act":
                P.op("act", lambda e: e.copy(out=dst, in_=pb), reads=[r_bank[bk]], writes=[r_dst])
            else:
                P.op("dve", lambda e: e.tensor_copy(out=dst, in_=pb), reads=[r_bank[bk]], writes=[r_dst])

        def proj_group(bk, ncols, pairs, reads):
            def mm(e):
                ins = None
                n = len(pairs)
                for i, (l, r) in enumerate(pairs):
                    ins = e.matmul(banks[bk][:, 0:ncols], lhsT=l, rhs=r, start=(i == 0), stop=(i == n - 1))
                return ins
            P.op("pe", mm, reads=reads, writes=[r_bank[bk]])

        def dump_sbuf(name, ap, row0=None):
            if name in dumps:
                ro = P.res()
                P.fence()
                P.dma("sp", dumps[name] if row0 is None else dumps[name][row0], ap, writes=[ro])
                fin_res.append(ro)

        A.reset(0)
        catT = A.alloc([8, S], BF16)
        off_after_cat = A.off
        qkvT = A.alloc([12, S], BF16)
        off_after_qkv = A.off
        r_cat = [[P.res(f"cat{c}_{b}") for b in range(8)] for c in range(8)]

        A.reset(0)
        w_a = A.alloc([8, 1536], BF16)
        g1bc = A.alloc([D], F32)
        hT = [A.alloc([8, 512], BF16) for _ in range(2)]
        xin = [A.alloc([D], F32) for _ in range(2)]
        xs = [A.alloc([D], BF16) for _ in range(2)]
        junk = A.alloc([D], BF16)
        r_wa = P.res("w_a")
        r_g1 = P.res("g1")
        r_hT = [[P.res() for _ in range(4)] for _ in range(2)]
        r_xin = [P.res() for _ in range(2)]
        r_xs = [P.res() for _ in range(2)]
        w_in_v = w_in_d.rearrange("(kc p) f -> p kc f", p=128)
        r_wa = [P.res() for _ in range(3)]
        for i in range(3):
            P.dma("pool", w_a[:, :, i * 512:(i + 1) * 512], w_in_v[:, :, i * 512:(i + 1) * 512], writes=[r_wa[i]])
        P.dma("sp", g1bc, g1_d, writes=[r_g1])
        ast = dict(tcount=0, pcount=0)

        def a_norm_tile(b, j):
            t = b * 4 + j
            sb = j % 2
            norm_tile(x_d[t * 128:(t + 1) * 128, :], g1bc, xin[sb], r_xin[sb], xs[sb], r_xs[sb], sb, junk, r_g1)

        def a_tr_tile(b, j):
            sb = j % 2
            transpose_tile(xs[sb], r_xs[sb], 0, hT[b % 2][:, :, j * 128:(j + 1) * 128], r_hT[b % 2][j],
                           "act" if j % 2 == 0 else "dve")

        for j in range(4):
            a_norm_tile(0, j)
            a_tr_tile(0, j)
        for b in range(8):
            hb = b % 2
            for fc in range(12):
                pcount = ast['pcount']
                bk = 1 + pcount % 3
                pairs = [(w_a[:, kc, fc * 128:(fc + 1) * 128], hT[hb][:, kc, :]) for kc in range(8)]
                proj_group(bk, 512, pairs, reads=[r_wa[fc // 4]] + r_hT[hb])
                dst = qkvT[:, fc, b * 512:(b + 1) * 512]
                if pcount % 2 == 0:
                    P.op("act", lambda e, dst=dst, bk=bk: e.copy(out=dst, in_=banks[bk][:]), reads=[r_bank[bk]])
                else:
                    P.op("dve", lambda e, dst=dst, bk=bk: e.tensor_copy(out=dst, in_=banks[bk][:]), reads=[r_bank[bk]])
                ast['pcount'] += 1
                if b + 1 < 8:
                    if fc % 3 == 0:
                        a_norm_tile(b + 1, fc // 3)
                    elif fc % 3 == 2:
                        a_tr_tile(b + 1, fc // 3)
        P.fence()
        if "qkvT" in dumps:
            for fc in range(12):
                dump_sbuf("qkvT", qkvT[:, fc, :], row0=fc)
        if stop_after == "A":
            return _finish(nc, P, fin_res)

        A.reset(32 * 1024)
        Vlay = [A.alloc([32, 128], BF16) for _ in range(2)]
        acc = A.alloc([S], F32)
        assert A.off <= off_after_cat
        A.reset(off_after_qkv)
        Pt = [A.alloc([256], BF16) for _ in range(4)]
        rden = A.alloc([2, 512], F32)
        rsh = A.alloc([2, 512], F32)
        sqb = A.alloc([4, 512], BF16)
        rsb = A.alloc([512], F32)
        rstdb = A.alloc([512], F32)
        r_V = [P.res("V0"), P.res("V1")]
        Pt.append(rsb.bitcast(BF16)[:, 0:256])
        Pt.append(rstdb.bitcast(BF16)[:, 0:256])
        r_Pt = [P.res() for _ in range(6)]
        r_accpat = P.res("accpat")
        r_rden = [P.res(), P.res()]
        r_rsh = [P.res(), P.res()]
        r_accn = [P.res() for _ in range(8)]
        tmpb = A.alloc([2, 512], BF16)
        r_tmpb = [P.res(), P.res()]
        kz = A.alloc([S], BF16)
        r_kz = [P.res() for _ in range(8)]
        P.op("pool", lambda e: e.memset(Vlay[0][:, :, 64:128], 1.0), writes=[r_V[0]])
        P.op("pool", lambda e: e.memset(Vlay[1][:, :, 64:128], 1.0), writes=[r_V[1]])
        BK_S = (0, 1, 2, 3, 4)
        BK_O = (5, 6, 7)
        s_cnt = [0]
        NH = int(os.environ.get('K_BHEADS', 8))
        tasks = []
        for h in range(NH):
            for p, d in enumerate(PATTERNS):
                nb = 32 // d
                for r_ in range(d):
                    for n_ in range(nb):
                        tasks.append((h, p, d, nb, r_, n_))
        LA = 4
        g_state = [0]

        VBLA = 8

        def build_V(i):
            h, p, d, nb, r_, n_ = tasks[i]
            hp, hs = h // 2, (h % 2) * 64
            vb = (h * 3 + p) % 2
            L, r_L = Vlay[vb], r_V[vb]
            vrow = qkvT[hs:hs + 64, 8 + hp, :]
            if True:
                idh = ident[hs:hs + 64, hs:hs + 64]
                for half in range(2):
                    BK_V = BK_S[s_cnt[0] % 5]
                    s_cnt[0] += 1
                    pv = bank_bf(BK_V).rearrange("p (a b) -> p a b", b=64)
                    def trv(e, half=half, pv=pv):
                        ins = None
                        for ii in range(16):
                            s_ = half * 16 + ii
                            rr_, nn_ = s_ // nb, s_ % nb
                            ins = e.transpose(pv[:, ii, :], in_=vrow[:, _sl(128 * d * nn_ + rr_, 128, d)], identity=idh)
                        return ins
                    P.op("pe", trv, writes=[r_bank[BK_V]])
                    dst = L[:, half * 16:(half + 1) * 16, 0:64]
                    if half == 0:
                        P.op("act", lambda e, dst=dst, pv=pv: e.copy(out=dst, in_=pv), reads=[r_bank[BK_V]], writes=[r_L])
                    else:
                        P.op("dve", lambda e, dst=dst, pv=pv: e.tensor_copy(out=dst, in_=pv), reads=[r_bank[BK_V]], writes=[r_L])

        def att_stage1(i):
            if i == 0:
                build_V(0)
            j = i + VBLA
            if j < len(tasks) and tasks[j][4] == 0 and tasks[j][5] == 0:
                build_V(j)
            h, p, d, nb, r_, n_ = tasks[i]
            hp, hs = h // 2, (h % 2) * 64
            if p == 0 and n_ < 8:
                c = n_
                cs = slice(c * 512, (c + 1) * 512)
                P.op("dve", lambda e: e.tensor_copy(out=kz[hs:hs + 64, cs], in_=qkvT[hs:hs + 64, 4 + hp, cs]), writes=[r_kz[c]])
                P.op("pool", lambda e: e.memset(kz[64 - hs:128 - hs, cs], 0.0), writes=[r_kz[c]])
            qrow = qkvT[:, hp, :]
            krow = kz
            base = 128 * d * n_ + r_
            ncol = 256 if n_ + 1 < nb else 128
            bs = BK_S[s_cnt[0] % 5]
            s_cnt[0] += 1
            pt, r_pt = Pt[i % 6], r_Pt[i % 6]
            ksl = krow[:, _sl(base, 128, d)]
            qsl = qrow[:, _sl(base, ncol, d)]
            mk = mask[:, p * 8 + h, :]
            P.op("pe", lambda e: e.matmul(banks[bs][:, 0:ncol], lhsT=ksl, rhs=qsl, start=True, stop=True),
                 reads=r_kz[base // 512:(base + 127 * d) // 512 + 1], writes=[r_bank[bs]])
            P.op("act", lambda e: e.activation(out=pt[:, 0:ncol], in_=banks[bs][:, 0:ncol], func=AF.Exp, scale=0.125),
                 reads=[r_bank[bs]], writes=[r_pt])
            P.op("dve" if i % 2 == 1 else "pool",
                 lambda e: e.tensor_tensor(out=pt[:, 0:ncol], in0=pt[:, 0:ncol], in1=mk[:, 0:ncol], op=ALU.mult),
                 reads=[r_pt], writes=[r_pt])

        def att_stage2(i):
            h, p, d, nb, r_, n_ = tasks[i]
            if p == 0 and pending_norm and n_ % 4 == 0:
                while pending_norm and pending_norm[0][1] <= n_ // 4:
                    emit_norm()
            hp, hs = h // 2, (h % 2) * 64
            vb = (h * 3 + p) % 2
            L, r_L = Vlay[vb], r_V[vb]
            base = 128 * d * n_ + r_
            pt, r_pt = Pt[i % 6], r_Pt[i % 6]
            slot = r_ * nb + n_
            g = g_state[0]
            g_state[0] += 1
            bo = BK_O[g % 3]
            P.op("pe", lambda e: e.matmul(banks[bo][:, 0:128], lhsT=L[:, slot, :], rhs=pt[:, 0:128],
                                          start=(n_ == 0), stop=True, skip_group_check=True),
                 reads=[r_pt, r_L], writes=[r_bank[bo]])
            if n_ + 1 < nb:
                bo2 = BK_O[(g + 1) % 3]
                P.op("pe", lambda e: e.matmul(banks[bo2][:, 0:128], lhsT=L[:, slot, :], rhs=pt[:, 128:256],
                                              start=True, stop=False, skip_group_check=True),
                     reads=[r_pt, r_L], writes=[r_bank[bo2]])
            av = acc[:, _sl(base, 128, d)]
            first_acc = (r_ == 0 and n_ == 0)
            if p == 0:
                rw = dict(reads=[r_bank[bo], r_accpat], writes=[r_accn[base // 512]])
            elif first_acc:
                rw = dict(reads=[r_bank[bo]], writes=[r_accpat])
            else:
                rw = dict(reads=[r_bank[bo], r_accpat])
            if p == 0:
                P.op("dve", lambda e: e.tensor_copy(out=av, in_=banks[bo][:, 0:128]), **rw)
            else:
                P.op("dve", lambda e: e.tensor_tensor(out=av, in0=banks[bo][:, 0:128], in1=av, op=ALU.add), **rw)
            last_of_head = (p == len(PATTERNS) - 1 and r_ == d - 1 and n_ == nb - 1)
            if last_of_head:
                for tb in range(8):
                    pending_norm.append((h, tb))
                emit_norm()

        pending_norm = []

        def emit_norm():
            h, tb = pending_norm.pop(0)
            hp, hs = h // 2, (h % 2) * 64
            cs = slice(tb * 512, (tb + 1) * 512)
            u = tb % 2
            rw = dict(reads=[r_accn[tb]], writes=[r_accpat, r_rden[u]]) if tb == 0 else dict(reads=[r_accpat, r_accn[tb]], writes=[r_rden[u]])
            P.op("act", lambda e: e.activation(out=rden[64:128, u, :], in_=acc[64:128, cs], func=AF.Ln), **rw)
            P.op("act", lambda e: e.activation(out=rsh[0:64, u, :], in_=rden[64:128, u, :], func=AF.Exp, scale=-1.0),
                 reads=[r_rden[u]], writes=[r_rsh[u]])
            eng = "pool" if tb % 2 == 0 else "dve"
            if hs == 0:
                P.op(eng, lambda e: e.tensor_tensor(out=catT[0:64, hp, cs], in0=acc[0:64, cs], in1=rsh[0:64, u, :], op=ALU.mult),
                     reads=[r_rsh[u], r_accpat, r_accn[tb]], writes=[])
            else:
                P.op(eng, lambda e: e.tensor_tensor(out=tmpb[0:64, u, :], in0=acc[0:64, cs], in1=rsh[0:64, u, :], op=ALU.mult),
                     reads=[r_rsh[u], r_accpat, r_accn[tb]], writes=[r_tmpb[u]])
                P.op("act", lambda e: e.copy(out=catT[64:128, hp, cs], in_=tmpb[0:64, u, :]), reads=[r_tmpb[u]], writes=[])

        for i in range(len(tasks) + LA):
            if i < len(tasks):
                att_stage1(i)
            if i - LA >= 0:
                att_stage2(i - LA)
        while pending_norm:
            emit_norm()
        P.fence()
        if stop_after == "B0":
            if "catT" in dumps:
                for c in range(8):
                    dump_sbuf("catT", catT[:, c, :], row0=c)
            return _finish(nc, P, fin_res)
        _save_off = A.off
        A.reset(off_after_cat)
        w_h = A.alloc([8, 2048], BF16)
        A.reset(_save_off)
        r_wh = [P.res() for _ in range(4)]
        for i in (2, 0, 3, 1):
            P.dma("pool", w_h[:, :, i * 512:(i + 1) * 512], w_in_v[:, :, 1536 + i * 512:1536 + (i + 1) * 512], writes=[r_wh[i]])
        r_sqb = [P.res() for _ in range(4)]
        r_rsb = P.res()
        r_rstdb = P.res()
        for b in range(8):
            bs_ = slice(b * 512, (b + 1) * 512)
            for c in range(4):
                if c % 2 == 0:
                    P.op("act", lambda e, c=c, bs_=bs_: e.activation(out=sqb[:, c, :], in_=catT[:, c, bs_], func=AF.Square),
                         reads=[r_cat[c][b]], writes=[r_sqb[c]])
                else:
                    P.op("dve", lambda e, c=c, bs_=bs_: e.tensor_tensor(out=sqb[:, c, :], in0=catT[:, c, bs_], in1=catT[:, c, bs_], op=ALU.mult),
                         reads=[r_cat[c][b]], writes=[r_sqb[c]])
            bk = 1 + b % 2
            proj_group(bk, 512, [(ones_bf[:], sqb[:, c, :]) for c in range(4)], reads=r_sqb)
            P.op("act", lambda e, bk=bk: e.activation(out=rsb, in_=banks[bk][:], func=AF.Ln, bias=EPS, scale=1.0 / 512),
                 reads=[r_bank[bk]], writes=[r_rsb])
            P.op("act", lambda e: e.activation(out=rstdb, in_=rsb, func=AF.Exp, scale=-0.5), reads=[r_rsb], writes=[r_rstdb])
            for c in range(4):
                P.op("dve", lambda e, c=c, bs_=bs_: e.scalar_tensor_tensor(out=catT[:, c, bs_], in0=catT[:, c, bs_], scalar=ga[:, c:c + 1],
                                                                       in1=rstdb, op0=ALU.mult, op1=ALU.mult),
                     reads=[r_rstdb], writes=[r_cat[c][b]])
        P.fence(skip_dma_queues=("pool",))
        if stop_after == "B":
            if "catT" in dumps:
                for c in range(8):
                    dump_sbuf("catT", catT[:, c, :], row0=c)
            return _finish(nc, P, fin_res)

        A.reset(off_after_cat)
        w_h = A.alloc([8, 2048], BF16)
        g1bc = A.alloc([D], F32)
        hT = [A.alloc([8, 512], BF16) for _ in range(2)]
        xin = [A.alloc([D], F32) for _ in range(2)]
        xs = [A.alloc([D], BF16) for _ in range(2)]
        junk = A.alloc([D], BF16)
        Vtok = [A.alloc([4, 512], BF16) for _ in range(2)]
        zeros = A.alloc([512], F32)

        def hbuf(dt, n=2):
            bl = [A.alloc([512], dt) for _ in range(n)]
            return bl if n == 2 else [bl[0], bl[0]]
        qs_b, Pc_b, sg_b = [hbuf(F32) for _ in range(3)]
        f_b, kk_b, R_b = [hbuf(F32, 1) for _ in range(3)]
        Qd_b, Kd_b, Kd2_b = [hbuf(BF16) for _ in range(3)]
        gs_b = [A.alloc([512], BF16) for _ in range(3)]
        r_gs3 = [P.res() for _ in range(3)]
        Kd2tok_b = [A.alloc([4, 128], BF16) for _ in range(2)]
        Am_b = [A.alloc([4, 128], BF16) for _ in range(2)]
        sqh_b = hbuf(BF16)
        rsh_b = hbuf(F32)
        y_b = hbuf(F32)
        r_g1 = P.res("g1")
        r_hT = [[P.res() for _ in range(4)] for _ in range(2)]
        r_xin = [P.res() for _ in range(2)]
        r_xs = [P.res() for _ in range(2)]
        r_Vtok = [[P.res() for _ in range(4)] for _ in range(2)]
        r_S = [[P.res(), P.res()] for hd in range(4)]
        r_Sbf = [[P.res() for _ in range(9)] for _ in range(4)]
        r_lb = P.res("lb")
        r_zero = P.res("zero")

        def rr(n):
            return [[P.res() for _ in range(2)] for _ in range(n)]
        def share(r):
            r[0][1] = r[0][0]
        (r_qs, r_sg, r_f, r_kk, r_Pc, r_R, r_E, r_Qd, r_Kd, r_Kd2, r_gs, r_Kd2tok, r_Am, r_sqh, r_rsh2, r_y) = [[x] for x in rr(16)]
        for r_ in (r_f, r_kk, r_R):
            share(r_)

        P.dma("sp", g1bc, g1_d, writes=[r_g1])
        P.op("pool", lambda e: e.memset(zeros, 0.0), writes=[r_zero])
        P.op("pool", lambda e: e.memset(Sst[:].rearrange("p a b c -> p (a b c)"), 0.0), writes=[r_S[hd][0] for hd in range(4)])
        P.op("pool", lambda e: e.memset(pshare[:, 0:4 * 9 * 128], 0.0), writes=[r_Sbf[hd][0] for hd in range(4)])
        lv = lbl[:].rearrange("p (h l) -> p h l", l=2)
        P.op("dve", lambda e: e.tensor_tensor(out=lbv[:, 12:16], in0=lv[:, :, 0], in1=lv[:, :, 1], op=ALU.subtract), writes=[r_lb])
        P.op("act", lambda e: e.activation(out=lbv[:, 0:4], in_=lbv[:, 12:16], func=AF.Sigmoid), reads=[r_lb], writes=[r_lb])
        P.op("act", lambda e: e.activation(out=lbv[:, 4:8], in_=lbv[:, 12:16], func=AF.Sigmoid, scale=-1.0), reads=[r_lb], writes=[r_lb])
        P.op("dve", lambda e: e.tensor_scalar(out=lbv[:, 8:12], in0=lbv[:, 4:8], scalar1=-1.0, scalar2=0.0, op0=ALU.mult, op1=ALU.add),
             reads=[r_lb], writes=[r_lb])
        BK_T, BK_P, BK_A, BK_U, BK_O2 = 0, (1, 2), 3, (5, 6), (4, 7)
        NBLK = int(os.environ.get('K_CBLOCKS', 8))
        cst = dict(tcount=0, pc=0)

        def bp_norm(b, j):
            t = b * 4 + j
            sb = j % 2
            norm_tile(x_d[t * 128:(t + 1) * 128, :], g1bc, xin[sb], r_xin[sb], xs[sb], r_xs[sb], sb, junk, r_g1)

        def bp_tr(b, j):
            sb = j % 2
            transpose_tile(xs[sb], r_xs[sb], BK_T, hT[b % 2][:, :, j * 128:(j + 1) * 128], r_hT[b % 2][j],
                           "act" if j % 2 == 0 else "dve")

        def bp_vtok(b):
            hb = b % 2
            for j in range(4):
                bk = BK_P[cst['pc'] % 2]
                cst['pc'] += 1
                pairs = [(hT[hb][:, kc, j * 128:(j + 1) * 128], w_h[:, kc, 1024:1536]) for kc in range(8)]
                proj_group(bk, 512, pairs, reads=[r_wh[2], r_hT[hb][j]])
                P.op("dve", lambda e, bk=bk, j=j: e.tensor_copy(out=Vtok[hb][:, j, :], in_=banks[bk][:]),
                     reads=[r_bank[bk]], writes=[r_Vtok[hb][j]])

        def block_prep(b, stage=None):
            if stage is None:
                for j in range(4):
                    bp_norm(b, j)
                    bp_tr(b, j)
                bp_vtok(b)
            elif stage == 0:
                bp_norm(b, 0)
                bp_norm(b, 1)
            elif stage == 1:
                bp_tr(b, 0)
                bp_tr(b, 1)
                bp_norm(b, 2)
                bp_norm(b, 3)
            else:
                bp_tr(b, 2)
                bp_tr(b, 3)
                bp_vtok(b)

        def hg_A(k):
            b, hd = divmod(k, 4)
            hb, w = b % 2, k % 2
            qs, sg, gs = qs_b[w], sg_b[w], gs_b[k % 3]
            for col0, func, dst, r_dst in ((0, AF.Silu, qs, r_qs[0][w]), (1536, AF.Silu, gs, r_gs3[k % 3]), (512, AF.Sigmoid, sg, r_sg[0][w])):
                bk = BK_P[cst['pc'] % 2]
                cst['pc'] += 1
                c0 = col0 + hd * 128
                pairs = [(w_h[:, kc, c0:c0 + 128], hT[hb][:, kc, :]) for kc in range(8)]
                proj_group(bk, 512, pairs, reads=[r_wh[col0 // 512]] + r_hT[hb])
                P.op("act", lambda e, bk=bk, dst=dst, func=func: e.activation(out=dst, in_=banks[bk][:], func=func),
                     reads=[r_bank[bk]], writes=[r_dst])

        def hg_B(k):
            b, hd = divmod(k, 4)
            hb, w = b % 2, k % 2
            qs, sg, f_, kk, Pc, R_ = qs_b[w], sg_b[w], f_b[w], kk_b[w], Pc_b[w], R_b[w]
            Qd, Kd, Kd2 = Qd_b[w], Kd_b[w], Kd2_b[w]
            Kd2tok, Am = Kd2tok_b[w], Am_b[w]
            P.op("dve", lambda e: e.tensor_scalar(out=f_, in0=sg, scalar1=lbv[:, 4 + hd:5 + hd], scalar2=lbv[:, hd:hd + 1],
                                                  op0=ALU.mult, op1=ALU.add),
                 reads=[r_sg[0][w], r_lb], writes=[r_f[0][w]])
            P.op("pool", lambda e: e.tensor_scalar(out=kk, in0=sg, scalar1=lbv[:, 8 + hd:9 + hd], scalar2=lbv[:, 4 + hd:5 + hd],
                                                   op0=ALU.mult, op1=ALU.add),
                 reads=[r_sg[0][w], r_lb], writes=[r_kk[0][w]])

            def scan(e):
                ins = None
                for c in range(8):
                    cs = slice(c * 64, (c + 1) * 64)
                    ins = e.tensor_tensor_scan(out=Pc[:, cs], data0=f_[:, cs], data1=zeros[:, cs], initial=1.0,
                                               op0=ALU.mult, op1=ALU.add)
                return ins
            P.op("dve", scan, reads=[r_f[0][w], r_zero], writes=[r_Pc[0][w]])
            P.op("dve", lambda e: e.reciprocal(out=R_, in_=Pc), reads=[r_Pc[0][w]], writes=[r_R[0][w]])
            Pc3 = Pc.rearrange("p (c j) -> p c j", j=64)
            P.op("pool", lambda e: e.tensor_tensor(out=Qd, in0=qs, in1=Pc, op=ALU.mult),
                 reads=[r_qs[0][w], r_Pc[0][w]], writes=[r_Qd[0][w]])
            P.op("dve", lambda e: e.tensor_tensor(out=Kd, in0=kk, in1=R_, op=ALU.mult),
                 reads=[r_kk[0][w], r_R[0][w]], writes=[r_Kd[0][w]])
            P.op("pool", lambda e: e.tensor_tensor(out=Kd2.rearrange("p (c j) -> p c j", j=64), in0=Kd.rearrange("p (c j) -> p c j", j=64),
                                                   in1=Pc3[:, :, 63:64].to_broadcast([128, 8, 64]), op=ALU.mult),
                 reads=[r_Kd[0][w], r_Pc[0][w]], writes=[r_Kd2[0][w]])
            if k >= 1:
                hg_C2(k - 1)
            pt2 = bank_bf(BK_T).rearrange("p (a b) -> p a b", b=128)

            def trk(e):
                ins = None
                for j in range(4):
                    ins = e.transpose(pt2[:, j, :], in_=Kd2[:, j * 128:(j + 1) * 128], identity=ident[:])
                return ins
            P.op("pe", trk, reads=[r_Kd2[0][w]], writes=[r_bank[BK_T]])
            P.op("act", lambda e: e.copy(out=Kd2tok, in_=pt2[:, 0:4, :]), reads=[r_bank[BK_T]], writes=[r_Kd2tok[0][w]])
            pa = banks[BK_A][:].rearrange("p (a b) -> p a b", b=128)

            def mma(e):
                ins = None
                for j in range(4):
                    js = slice(j * 128, (j + 1) * 128)
                    ins = e.matmul(pa[:, j, :], lhsT=Kd[:, js], rhs=Qd[:, js], start=True, stop=True)
                return ins
            P.op("pe", mma, reads=[r_Kd[0][w], r_Qd[0][w]], writes=[r_bank[BK_A]])
            P.op("dve", lambda e: e.tensor_tensor(out=Am, in0=pa, in1=mbd[:].unsqueeze(1).to_broadcast([128, 4, 128]), op=ALU.mult),
                 reads=[r_bank[BK_A]], writes=[r_Am[0][w]])
            for hf in range(2):
                bu = BK_U[hf]
                pu = banks[bu][:].rearrange("p (a b) -> p a b", b=128)

                def mmu(e, hf=hf, pu=pu):
                    ins = None
                    ps_ = slice(hf * 64, hf * 64 + 64)
                    for j in range(4):
                        ins = e.matmul(pu[:, j, :], lhsT=Kd2tok[ps_, j, :], rhs=Vtok[hb][ps_, j, hd * 128:(hd + 1) * 128],
                                       start=True, stop=True)
                    return ins
                P.op("pe", mmu, reads=[r_Kd2tok[0][w]] + r_Vtok[hb], writes=[r_bank[bu]])
            bo = BK_O2[w]
            po = banks[bo]

            def mmi(e):
                ins = None
                for j in range(4):
                    ins = e.matmul(po[:, j * 128:(j + 1) * 128], lhsT=Vtok[hb][:, j, hd * 128:(hd + 1) * 128], rhs=Am[:, j, :],
                                   start=(j == 0), stop=False, skip_group_check=True)
                return ins
            P.op("pe", mmi, reads=[r_Am[0][w]] + r_Vtok[hb], writes=[r_bank[bo]])
            for c in range(8):
                m = b * 8 + c
                bu = BK_U[c % 2]
                pu = banks[bu][:].rearrange("p (a b) -> p a b", b=128)
                src, dst = Sst[:, hd, m % 2, :], Sst[:, hd, (m + 1) % 2, :]
                P.op("dve", lambda e, c=c, pu=pu, src=src, dst=dst:
                     e.scalar_tensor_tensor(out=dst, in0=src, scalar=Pc[:, c * 64 + 63:c * 64 + 64],
                                            in1=pu[:, c // 2, :], op0=ALU.mult, op1=ALU.add),
                     reads=[r_bank[bu], r_Pc[0][w], r_S[hd][m % 2]], writes=[r_S[hd][(m + 1) % 2]])
                P.op("act", lambda e, m=m, dst=dst: e.copy(out=Sbf[:, hd, (m + 1) % 9, :], in_=dst),
                     reads=[r_S[hd][(m + 1) % 2]], writes=[r_Sbf[hd][(m + 1) % 9]])

        def hg_C1(k):
            b, hd = divmod(k, 4)
            w = k % 2
            Qd = Qd_b[w]
            sqh, rs2 = sqh_b[w], rsh_b[w]
            bo = BK_O2[w]
            po = banks[bo]

            def mmx(e):
                ins = None
                for c in range(8):
                    ins = e.matmul(po[:, c * 64:(c + 1) * 64], lhsT=Sbf[:, hd, (b * 8 + c) % 9, :], rhs=Qd[:, c * 64:(c + 1) * 64],
                                   start=False, stop=(c == 7), skip_group_check=True)
                return ins
            P.op("pe", mmx, reads=[r_Sbf[hd][(b * 8 + c) % 9] for c in range(8)] + [r_Qd[0][w]], writes=[r_bank[bo]])
            P.op("act", lambda e: e.activation(out=sqh, in_=po[:], func=AF.Square), reads=[r_bank[bo]], writes=[r_sqh[0][w]])
            bk = BK_P[cst['pc'] % 2]
            cst['pc'] += 1
            proj_group(bk, 512, [(ones_bf[:], sqh)], reads=[r_sqh[0][w]])
            P.op("act", lambda e: e.activation(out=rs2, in_=banks[bk][:], func=AF.Ln, bias=EPS, scale=1.0 / 128),
                 reads=[r_bank[bk]], writes=[r_rsh2[0][w]])
            P.op("act", lambda e: e.activation(out=rs2, in_=rs2, func=AF.Exp, scale=-0.5),
                 reads=[r_rsh2[0][w]], writes=[r_rsh2[0][w]])

        def hg_C2(k):
            b, hd = divmod(k, 4)
            w = k % 2
            bs_ = slice(b * 512, (b + 1) * 512)
            gs = gs_b[k % 3]
            rs2, y_ = rsh_b[w], y_b[w]
            bo = BK_O2[w]
            po = banks[bo]
            P.op("dve", lambda e: e.tensor_tensor(out=y_, in0=po[:], in1=rs2, op=ALU.mult),
                 reads=[r_bank[bo], r_rsh2[0][w]], writes=[r_y[0][w]])
            P.op("dve", lambda e: e.scalar_tensor_tensor(out=catT[:, 4 + hd, bs_], in0=y_, scalar=gh[:, hd:hd + 1], in1=gs,
                                                         op0=ALU.mult, op1=ALU.mult),
                 reads=[r_y[0][w], r_gs3[k % 3]], writes=[])

        block_prep(0)
        NK = NBLK * 4
        hg_A(0)
        for k in range(NK):
            b, hd = divmod(k, 4)
            if hd < 3 and b + 1 < NBLK:
                block_prep(b + 1, stage=hd)
            if k + 1 < NK:
                hg_A(k + 1)
            if k >= 1:
                hg_C1(k - 1)
            hg_B(k)
        hg_C1(NK - 1)
        hg_C2(NK - 1)
        P.muted = False
        P.fence()
        if "catT" in dumps:
            for c in range(8):
                dump_sbuf("catT", catT[:, c, :], row0=c)
        if stop_after == "C":
            return _finish(nc, P, fin_res)

        KB = 1024
        A.reset(off_after_cat)
        w_o = A.alloc([8, D], BF16)
        gpbc = A.alloc([D], F32)
        xin = [A.alloc([D], F32) for _ in range(2)]
        mix = [A.alloc([D], F32) for _ in range(2)]
        junk = A.alloc([D], BF16)
        assert A.off <= 104 * KB
        A.reset(104 * KB)
        w1 = A.alloc([8, DFF], BF16)
        xin += [A.alloc([D], F32) for _ in range(2)]
        mix += [A.alloc([D], F32) for _ in range(2)]
        NDB = 4
        r_wo = [P.res(), P.res()]
        r_catD = P.res()
        r_x1s = [P.res() for _ in range(32)]
        r_w1 = [P.res() for _ in range(8)]
        r_gp = P.res("gp")
        r_xin = [P.res() for _ in range(NDB)]
        r_mix = [P.res() for _ in range(NDB)]
        w_out_v = w_out_d.rearrange("(kc p) f -> p kc f", p=128)
        w1_v = w1_d.rearrange("(kc p) f -> p kc f", p=128)
        w2_v = w2_d.rearrange("(kc p) f -> p kc f", p=128)
        P.dma("pool", w_o[:, :, 0:512], w_out_v[:, :, 0:512], writes=[r_wo[0]])
        P.dma("pool", w_o[:, :, 512:1024], w_out_v[:, :, 512:1024], writes=[r_wo[1]])
        P.dma("sp", gpbc, gpost_d, writes=[r_gp])
        for t in range(32):
            sb = t % NDB
            ts_ = slice(t * 128, (t + 1) * 128)
            if t % 3 == 1 and t // 3 < 8:
                i = t // 3
                P.dma("pool", w1[:, :, i * 512:(i + 1) * 512], w1_v[:, :, i * 512:(i + 1) * 512], writes=[r_w1[i]])
            P.dma("sp", xin[sb], x_d[ts_, :], writes=[r_xin[sb]])
            for half in range(2):
                bk = (t % 4) * 2 + half
                pairs = [(catT[:, kc, ts_], w_o[:, kc, half * 512:(half + 1) * 512]) for kc in range(8)]
                proj_group(bk, 512, pairs, reads=[r_wo[half], r_catD])
                P.op("act", lambda e, bk=bk, sb=sb, half=half: e.copy(out=mix[sb][:, half * 512:(half + 1) * 512], in_=banks[bk][:]),
                     reads=[r_bank[bk]], writes=[r_mix[sb]])
            _post_norm_residual(P, mix[sb], r_mix[sb], xin[sb], r_xin[sb], gpbc, r_gp, stat, r_stat[sb], sb, junk)
            P.dma("pool", x1s_d[ts_, :], mix[sb], reads=[r_mix[sb]], writes=[r_x1s[t]])
        w2 = arena_t[:, 0:(32 * D)].rearrange("p (a b) -> p a b", b=D)
        r_w2 = [P.res() for _ in range(8)]
        for i in range(8):
            P.dma("pool", w2[:, i * 4:(i + 1) * 4, :], w2_v[:, i * 4:(i + 1) * 4, :], writes=[r_w2[i], r_catD] if i == 0 else [r_w2[i]])
        P.fence(skip_dma_queues=("pool",))
        if stop_after == "D":
            return _finish(nc, P, fin_res)

        A.reset(64 * KB)
        g2bc = A.alloc([D], F32)
        gp2bc = A.alloc([D], F32)
        xin = [[A.alloc([D], F32) for _ in range(2)] for _ in range(2)]
        xs = [A.alloc([D], BF16) for _ in range(2)]
        h2T = [A.alloc([8, 256], BF16) for _ in range(2)]
        ffo = A.alloc([D], F32)
        assert A.off <= 104 * KB, A.off
        A.reset(168 * KB)
        uT = A.alloc([32, 256], BF16)
        rl2 = A.alloc([2, 512], BF16)
        rl = [rl2[:, 0, :], rl2[:, 1, :]]
        junk = A.alloc([D], BF16)
        r_g2 = P.res("g2")
        r_gp2 = P.res("gp2")
        r_xin = [[P.res() for _ in range(2)] for _ in range(2)]
        r_xs = [P.res() for _ in range(2)]
        r_h2T = [[P.res() for _ in range(2)] for _ in range(2)]
        r_uT = [P.res() for _ in range(16)]
        r_rl = [P.res() for _ in range(2)]
        r_ffo = P.res()
        P.dma("sp", g2bc, g2_d, writes=[r_g2])
        P.dma("sp", gp2bc, gpost2_d, writes=[r_gp2])
        BK_T = 0
        BK_F1 = (1, 2, 3)
        BK_F2 = (4, 5, 6, 7)
        NEB = 16

        def e_norm(b):
            for j in range(2):
                t = b * 2 + j
                norm_tile(x1s_d[t * 128:(t + 1) * 128, :], g2bc, xin[b % 2][j], r_xin[b % 2][j], xs[j], r_xs[j], j, junk, r_g2, src_reads=[r_x1s[t]])

        def e_tr(b):
            for j in range(2):
                transpose_tile(xs[j], r_xs[j], BK_T, h2T[b % 2][:, :, j * 128:(j + 1) * 128], r_h2T[b % 2][j],
                               "act" if j == 0 else "dve")

        est = dict(f1c=0)

        def e_ffn1(b, lo=0, hi=16):
            hb = b % 2
            for fp in range(lo, hi):
                bk = BK_F1[est['f1c'] % 3]
                w = est['f1c'] % 2
                est['f1c'] += 1
                pb = banks[bk][:].rearrange("p (a b) -> p a b", b=256)

                def mm1(e, pb=pb, fp=fp):
                    ins = None
                    for i in range(2):
                        fc = fp * 2 + i
                        for kc in range(8):
                            ins = e.matmul(pb[:, i, :], lhsT=w1[:, kc, fc * 128:(fc + 1) * 128], rhs=h2T[hb][:, kc, :],
                                           start=(kc == 0 and i == 0), stop=(kc == 7), skip_group_check=True)
                    return ins
                P.op("pe", mm1, reads=[r_w1[fp // 2]] + r_h2T[hb], writes=[r_bank[bk]])
                P.op("act", lambda e, w=w, bk=bk: e.activation(out=rl[w], in_=banks[bk][:], func=AF.Relu),
                     reads=[r_bank[bk]], writes=[r_rl[w]])
                P.op("dve" if (fp % 2 == 0 or b == 0) else "pool",
                     lambda e, w=w, fp=fp: e.tensor_tensor(out=uT[:, fp * 2:fp * 2 + 2, :], in0=rl[w].rearrange("p (a b) -> p a b", b=256),
                                                           in1=rl[w].rearrange("p (a b) -> p a b", b=256), op=ALU.mult),
                     reads=[r_rl[w]], writes=[r_uT[fp]])

        def e_ffn2(b):
            for j in range(2):
                t = b * 2 + j
                ts_ = slice(t * 128, (t + 1) * 128)
                xr, r_xr = xin[b % 2][j], r_xin[b % 2][j]
                for half in range(2):
                    bk = BK_F2[j * 2 + half]
                    for g8 in range(8):
                        def mm2(e, g8=g8, bk=bk, j=j, half=half):
                            ins = None
                            for fc in range(g8 * 4, g8 * 4 + 4):
                                ins = e.matmul(banks[bk][:, 0:512], lhsT=uT[:, fc, j * 128:(j + 1) * 128], rhs=w2[:, fc, half * 512:(half + 1) * 512],
                                               start=(fc == 0), stop=(fc == 31), skip_group_check=True)
                            return ins
                        P.op("pe", mm2, reads=[r_w2[g8], r_uT[g8 * 2], r_uT[g8 * 2 + 1]], writes=[r_bank[bk]])
                    P.op("act", lambda e, bk=bk, half=half: e.copy(out=ffo[:, half * 512:(half + 1) * 512], in_=banks[bk][:]),
                         reads=[r_bank[bk]], writes=[r_ffo])
                _post_norm_residual(P, ffo, r_ffo, xr, r_xr, gp2bc, r_gp2, stat, r_stat[2 + j], 2 + j, junk, into_res=True)
                ro = P.res()
                P.dma("pool", out_d[ts_, :], xr, reads=[r_xr], writes=[ro])
                fin_res.append(ro)

        e_norm(0)
        e_tr(0)
        for b in range(NEB):
            e_ffn1(b, 0, 8)
            if b + 1 < NEB:
                e_norm(b + 1)
            e_ffn1(b, 8, 16)
            if b + 1 < NEB:
                e_tr(b + 1)
            e_ffn2(b)
        return _finish(nc, P, fin_res)


def _post_norm_residual(P, buf, r_buf, xres, r_xres, gbc, r_g, stat, r_st, sb, junk, into_res=False):
    st0 = stat[:, 4 * sb:4 * sb + 1]
    st1 = stat[:, 4 * sb + 1:4 * sb + 2]
    st2 = stat[:, 4 * sb + 2:4 * sb + 3]
    P.op("act", lambda e: e.activation(out=junk, in_=buf, func=AF.Square, accum_out=st0), reads=[r_buf], writes=[r_st])
    P.op("act", lambda e: e.activation(out=st1, in_=st0, func=AF.Sqrt, bias=EPS, scale=1.0 / D), reads=[r_st], writes=[r_st])
    P.op("dve", lambda e: e.reciprocal(out=st2, in_=st1), reads=[r_st], writes=[r_st])
    P.op("dve", lambda e: e.scalar_tensor_tensor(out=buf, in0=buf, scalar=st2, in1=gbc, op0=ALU.mult, op1=ALU.mult),
         reads=[r_buf, r_st, r_g], writes=[r_buf])
    if into_res:
        P.op("pool", lambda e: e.tensor_tensor(out=xres, in0=buf, in1=xres, op=ALU.add), reads=[r_buf, r_xres], writes=[r_xres])
    else:
        P.op("pool", lambda e: e.tensor_tensor(out=buf, in0=buf, in1=xres, op=ALU.add), reads=[r_buf, r_xres], writes=[r_buf])


def _finish(nc, P, fin_res):
    P.fence()
    P.build()
    return nc


def _make_in_maps(inputs):
    f32 = np.float32
    x = np.ascontiguousarray(np.asarray(inputs["x"], dtype=f32))
    ident, mask, mbd = _const_tables()

    def bc(v):
        return np.ascontiguousarray(np.broadcast_to(np.asarray(v, dtype=f32).reshape(1, D), (128, D)))

    def fm(v):
        return np.ascontiguousarray(np.asarray(v, dtype=f32).reshape(4, 128).T)
    lbl = np.asarray(inputs["hgrn_lb_logits"], dtype=f32)
    lbl_l = np.ascontiguousarray(lbl.reshape(2, 4, 128).transpose(2, 1, 0).reshape(128, 8))
    shared = {
        "w_in": np.ascontiguousarray(np.asarray(inputs["w_in"], dtype=f32)[0]),
        "w_out": np.ascontiguousarray(np.asarray(inputs["w_out"], dtype=f32)[0]),
        "w_ff1": np.ascontiguousarray(np.asarray(inputs["w_ff1"], dtype=f32)[0]),
        "w_ff2": np.ascontiguousarray(np.asarray(inputs["w_ff2"], dtype=f32)[0]),
        "g1": bc(inputs["mix_pre_norm"][0]),
        "gpost": bc(inputs["mix_post_norm"][0]),
        "g2": bc(inputs["mlp_pre_norm"][0]),
        "gpost2": bc(inputs["mlp_post_norm"][0]),
        "ga": fm(inputs["attn_out_norm"][0]),
        "gh": fm(inputs["hgrn_out_norm"][0]),
        "lbl": lbl_l,
        "c_ident": ident,
        "c_mask": mask,
        "c_mbd": mbd,
    }
    return [dict(shared, x=x[b]) for b in range(NCORE)]


def kernel(**inputs):
    in_maps = _make_in_maps(inputs)
    nc = build_program()
    res = run_bass_kernel_spmd(nc, in_maps, core_ids=list(range(NCORE)))
    out = np.stack([np.asarray(r["out"]) for r in res.results], axis=0)
    return out.astype(np.float32)
```

```python
import contextlib
import os
import numpy as np
import ml_dtypes
import concourse.bass as bass
import concourse.mybir as mybir
from concourse.bass_utils import run_bass_kernel_spmd

F32 = mybir.dt.float32
BF16 = mybir.dt.bfloat16
AF = mybir.ActivationFunctionType
ALU = mybir.AluOpType

S = 4096
D = 1024
NCORE = 8
DFF = 4096
INW = 3584
EPS = 1e-6
PATTERNS = (1, 4, 16)
ARENA = 188 * 1024


class Res:
    __slots__ = ("name", "w", "r")

    def __init__(self, name):
        self.name = name
        self.w = None
        self.r = []


class Prog:
    ENG = ("pe", "act", "dve", "pool", "sp")

    def __init__(self, nc, stack, n_dma_slots=8):
        self.nc = nc
        self.stack = stack
        self.ops = {e: [] for e in self.ENG}
        self.sems = {}
        self.count = {}
        for e in self.ENG:
            self.sems[e] = stack.enter_context(nc.semaphore("s_" + e))
            self.count[e] = 0
        self.seen = {e: {} for e in self.ENG}
        self.dma_slots = {}
        self.n_dma_slots = n_dma_slots
        self.nres = 0
        self.muted = False

    def res(self, name=None):
        self.nres += 1
        return Res(name or f"r{self.nres}")

    def sbuf(self, name, shape, dtype):
        return self.stack.enter_context(self.nc.sbuf_tensor("sb_" + name, list(shape), dtype))

    def psum(self, name, shape, dtype=F32):
        return self.stack.enter_context(self.nc.psum_tensor(name, list(shape), dtype))

    def _collect(self, reads, writes):
        waits = {}

        def add(ev):
            if ev is None:
                return
            k, v = ev
            if waits.get(k, 0) < v:
                waits[k] = v
        for r in reads:
            add(r.w)
        for w in writes:
            add(w.w)
            for ev in w.r:
                add(ev)
        return waits

    def _emit_waits(self, eng, waits):
        seen = self.seen[eng]
        out = []
        for k, v in waits.items():
            if k == eng and eng in ("pe", "sp"):
                continue
            if seen.get(k, 0) < v:
                seen[k] = v
                out.append((self.sems[k], v))
        return out

    def op(self, eng, fn, reads=(), writes=()):
        if self.muted:
            return None
        wl = self._emit_waits(eng, self._collect(reads, writes))
        self.count[eng] += 1
        ev = (eng, self.count[eng])
        sem = self.sems[eng]

        def run(e, fn=fn, wl=wl, sem=sem):
            for s, v in wl:
                e.wait_ge(s, v)
            fn(e).then_inc(sem, 1)
        self.ops[eng].append(run)
        for r in reads:
            r.r.append(ev)
        for w in writes:
            w.w = ev
            w.r = []
        return ev

    def dma(self, queue, out, in_, reads=(), writes=()):
        if self.muted:
            return None
        if queue not in self.dma_slots:
            slots = []
            for i in range(self.n_dma_slots):
                name = f"d_{queue}{i}"
                self.sems[name] = self.stack.enter_context(self.nc.semaphore(name))
                slots.append([name, 0])
            self.dma_slots[queue] = [slots, 0]
        slots, idx = self.dma_slots[queue]
        slot = slots[idx % len(slots)]
        self.dma_slots[queue][1] = idx + 1
        waits = self._collect(reads, writes)
        if slot[1] > 0 and waits.get(slot[0], 0) < slot[1]:
            waits[slot[0]] = slot[1]
        wl = self._emit_waits(queue, waits)
        slot[1] += 16
        ev = (slot[0], slot[1])
        sem = self.sems[slot[0]]

        def run(e, wl=wl, sem=sem, out=out, in_=in_):
            for s, v in wl:
                e.wait_ge(s, v)
            e.dma_start(out=out, in_=in_).then_inc(sem, 16)
        self.ops[queue].append(run)
        for r in reads:
            r.r.append(ev)
        for w in writes:
            w.w = ev
            w.r = []
        return ev

    def fence(self, engines=None, skip_dma_queues=()):
        targets = {e: self.count[e] for e in ("pe", "act", "dve", "pool") if self.count[e] > 0}
        for q, (slots, idx) in self.dma_slots.items():
            if q in skip_dma_queues:
                continue
            for name, val in slots:
                if val > 0:
                    targets[name] = val
        for eng in (engines or self.ENG):
            wl = self._emit_waits(eng, dict(targets))
            if not wl:
                continue

            def run(e, wl=wl):
                for s, v in wl:
                    e.wait_ge(s, v)
            self.ops[eng].append(run)

    def build(self):
        with self.nc.Block() as block:
            @block.tensor
            def _(e):
                for f in self.ops["pe"]:
                    f(e)

            @block.scalar
            def _(e):
                for f in self.ops["act"]:
                    f(e)

            @block.vector
            def _(e):
                for f in self.ops["dve"]:
                    f(e)

            @block.gpsimd
            def _(e):
                for f in self.ops["pool"]:
                    f(e)

            @block.sync
            def _(e):
                for f in self.ops["sp"]:
                    f(e)


def _sl(base, n, d):
    return slice(base, base + (n - 1) * d + 1, d)


class Arena:
    def __init__(self, ap_bf16, nbytes):
        self.ap = ap_bf16
        self.n = nbytes
        self.off = 0

    def reset(self, off=0):
        self.off = off

    def alloc(self, free_shape, dtype):
        esz = 4 if dtype == F32 else 2
        n = int(np.prod(free_shape)) * esz
        self.off = (self.off + 63) // 64 * 64
        assert self.off + n <= self.n, f"arena overflow {self.off + n} > {self.n}"
        v = self.ap[:, self.off // 2:(self.off + n) // 2]
        if dtype == F32:
            v = v.bitcast(F32)
        self.off += n
        if len(free_shape) == 2:
            v = v.rearrange("p (a b) -> p a b", b=free_shape[1])
        elif len(free_shape) == 3:
            v = v.rearrange("p (a b c) -> p a b c", b=free_shape[1], c=free_shape[2])
        return v


def _const_tables():
    ident = np.eye(128, dtype=np.float32).astype(ml_dtypes.bfloat16)
    k = np.arange(128)[:, None].astype(np.float64)
    q = np.arange(128)[None, :].astype(np.float64)
    mask = np.zeros((128, 24, 256), dtype=np.float32)
    for p, d in enumerate(PATTERNS):
        for h in range(8):
            slope = 2.0 ** (-(h + 1))
            dist_cur = q - k
            cur = np.where(dist_cur >= 0, np.exp(-slope * d * np.maximum(dist_cur, 0)), 0.0)
            dist_prev = q + 128 - k
            prv = np.where(dist_prev <= 128, np.exp(-slope * d * dist_prev), 0.0)
            mask[:, p * 8 + h, 0:128] = cur
            mask[:, p * 8 + h, 128:256] = prv
    mask = mask.astype(ml_dtypes.bfloat16)
    s_ = np.arange(128)[:, None]
    t_ = np.arange(128)[None, :]
    mbd = ((s_ // 64 == t_ // 64) & (s_ <= t_)).astype(np.float32).astype(ml_dtypes.bfloat16)
    return ident, mask, mbd


def build_program(dump=None, stop_after="E"):
    nc = bass.Bass("TRN2", target_bir_lowering=False)

    def din(name, shape, dt=F32):
        return nc.dram_tensor(name, list(shape), dt, kind="ExternalInput").ap()

    x_d = din("x", [S, D])
    w_in_d = din("w_in", [D, INW])
    w_out_d = din("w_out", [D, D])
    w1_d = din("w_ff1", [D, DFF])
    w2_d = din("w_ff2", [DFF, D])
    g1_d = din("g1", [128, D])
    gpost_d = din("gpost", [128, D])
    g2_d = din("g2", [128, D])
    gpost2_d = din("gpost2", [128, D])
    ga_d = din("ga", [128, 4])
    gh_d = din("gh", [128, 4])
    lbl_d = din("lbl", [128, 8])
    ident_d = din("c_ident", [128, 128], BF16)
    mask_d = din("c_mask", [128, 24, 256], BF16)
    mbd_d = din("c_mbd", [128, 128], BF16)
    out_d = nc.dram_tensor("out", [S, D], F32, kind="ExternalOutput").ap()
    x1s_d = nc.dram_tensor("x1s", [S, D], F32).ap()
    dumps = {}
    if dump:
        for name, shape, dt in dump:
            dumps[name] = nc.dram_tensor("dbg_" + name, list(shape), dt, kind="ExternalOutput").ap()

    with contextlib.ExitStack() as st:
        P = Prog(nc, st)
        ident = P.sbuf("ident", [128, 128], BF16)
        pshare = P.sbuf("pshare", [128, 24 * 256], BF16)
        mask = pshare[:].rearrange("p (a b) -> p a b", b=256)
        mbd = P.sbuf("mbd", [128, 128], BF16)
        ones_bf = P.sbuf("ones_bf", [128, 128], BF16)
        ga = P.sbuf("ga", [128, 4], F32)
        gh = P.sbuf("gh", [128, 4], F32)
        lbl = P.sbuf("lbl", [128, 8], F32)
        lbv = P.sbuf("lbv", [128, 16], F32)
        stat = P.sbuf("stat", [128, 16], F32)
        Sst = P.sbuf("Sst", [128, 4, 2, 128], F32)
        Sbf = pshare[:, 0:4 * 9 * 128].rearrange("p (a b c) -> p a b c", b=9, c=128)
        arena_t = P.sbuf("arena", [128, ARENA // 2], BF16)
        A = Arena(arena_t[:], ARENA)
        banks = [P.psum(f"bank{i}", [128, 512], F32) for i in range(8)]
        r_bank = [P.res(f"bank{i}") for i in range(8)]
        r_const = P.res("const")
        fin_res = []

        P.dma("sp", ident[:], ident_d, writes=[r_const])
        P.dma("sp", mask, mask_d, writes=[r_const])
        P.dma("sp", mbd[:], mbd_d, writes=[r_const])
        P.dma("sp", ga[:], ga_d, writes=[r_const])
        P.dma("sp", gh[:], gh_d, writes=[r_const])
        P.dma("sp", lbl[:], lbl_d, writes=[r_const])
        P.op("pool", lambda e: e.memset(ones_bf[:], 1.0), writes=[r_const])
        P.fence()

        def bank_bf(i):
            return banks[i][:].bitcast(BF16)

        def norm_tile(src_ap, gbc, xin, r_xin, xs, r_xs, sb, junk, r_g=None, src_reads=()):
            st0 = stat[:, 4 * sb:4 * sb + 1]
            st1 = stat[:, 4 * sb + 1:4 * sb + 2]
            st2 = stat[:, 4 * sb + 2:4 * sb + 3]
            r_st = r_stat[sb]
            P.dma("sp", xin, src_ap, reads=list(src_reads), writes=[r_xin])
            P.op("act", lambda e: e.activation(out=junk, in_=xin, func=AF.Square, accum_out=st0),
                 reads=[r_xin], writes=[r_st])
            P.op("act", lambda e: e.activation(out=st1, in_=st0, func=AF.Sqrt, bias=EPS, scale=1.0 / D),
                 reads=[r_st], writes=[r_st])
            P.op("dve", lambda e: e.reciprocal(out=st2, in_=st1), reads=[r_st], writes=[r_st])
            P.op("dve", lambda e: e.scalar_tensor_tensor(out=xs, in0=xin, scalar=st2, in1=gbc,
                                                         op0=ALU.mult, op1=ALU.mult),
                 reads=[r_xin, r_st] + ([r_g] if r_g is not None else []), writes=[r_xs])

        r_stat = [P.res(f"stat{i}") for i in range(4)]

        def transpose_tile(xs, r_xs, bk, dst, r_dst, evac_eng):
            pb = bank_bf(bk).rearrange("p (a b) -> p a b", b=128)

            def tr(e):
                ins = None
                for kc in range(8):
                    ins = e.transpose(pb[:, kc, :], in_=xs[:, kc * 128:(kc + 1) * 128], identity=ident[:])
                return ins
            P.op("pe", tr, reads=[r_xs], writes=[r_bank[bk]])
            if evac_eng == "act":
                P.op("act", lambda e: e.copy(out=dst, in_=pb), reads=[r_bank[bk]], writes=[r_dst])
            else:
                P.op("dve", lambda e: e.tensor_copy(out=dst, in_=pb), reads=[r_bank[bk]], writes=[r_dst])

        def proj_group(bk, ncols, pairs, reads):
            def mm(e):
                ins = None
                n = len(pairs)
                for i, (l, r) in enumerate(pairs):
                    ins = e.matmul(banks[bk][:, 0:ncols], lhsT=l, rhs=r, start=(i == 0), stop=(i == n - 1))
                return ins
            P.op("pe", mm, reads=reads, writes=[r_bank[bk]])

        def dump_sbuf(name, ap, row0=None):
            if name in dumps:
                ro = P.res()
                P.fence()
                P.dma("sp", dumps[name] if row0 is None else dumps[name][row0], ap, writes=[ro])
                fin_res.append(ro)

        A.reset(0)
        catT = A.alloc([8, S], BF16)
        off_after_cat = A.off
        qkvT = A.alloc([12, S], BF16)
        off_after_qkv = A.off
        r_cat = [[P.res(f"cat{c}_{b}") for b in range(8)] for c in range(8)]

        A.reset(0)
        w_a = A.alloc([8, 1536], BF16)
        g1bc = A.alloc([D], F32)
        hT = [A.alloc([8, 512], BF16) for _ in range(2)]
        xin = [A.alloc([D], F32) for _ in range(2)]
        xs = [A.alloc([D], BF16) for _ in range(2)]
        junk = A.alloc([D], BF16)
        r_wa = P.res("w_a")
        r_g1 = P.res("g1")
        r_hT = [[P.res() for _ in range(4)] for _ in range(2)]
        r_xin = [P.res() for _ in range(2)]
        r_xs = [P.res() for _ in range(2)]
        w_in_v = w_in_d.rearrange("(kc p) f -> p kc f", p=128)
        r_wa = [P.res() for _ in range(3)]
        for i in range(3):
            P.dma("pool", w_a[:, :, i * 512:(i + 1) * 512], w_in_v[:, :, i * 512:(i + 1) * 512], writes=[r_wa[i]])
        P.dma("sp", g1bc, g1_d, writes=[r_g1])
        ast = dict(tcount=0, pcount=0)

        def a_norm_tile(b, j):
            t = b * 4 + j
            sb = j % 2
            norm_tile(x_d[t * 128:(t + 1) * 128, :], g1bc, xin[sb], r_xin[sb], xs[sb], r_xs[sb], sb, junk, r_g1)

        def a_tr_tile(b, j):
            sb = j % 2
            transpose_tile(xs[sb], r_xs[sb], 0, hT[b % 2][:, :, j * 128:(j + 1) * 128], r_hT[b % 2][j],
                           "act" if j % 2 == 0 else "dve")

        for j in range(4):
            a_norm_tile(0, j)
            a_tr_tile(0, j)
        for b in range(8):
            hb = b % 2
            for fc in range(12):
                pcount = ast['pcount']
                bk = 1 + pcount % 3
                pairs = [(w_a[:, kc, fc * 128:(fc + 1) * 128], hT[hb][:, kc, :]) for kc in range(8)]
                proj_group(bk, 512, pairs, reads=[r_wa[fc // 4]] + r_hT[hb])
                dst = qkvT[:, fc, b * 512:(b + 1) * 512]
                if pcount % 2 == 0:
                    P.op("act", lambda e, dst=dst, bk=bk: e.copy(out=dst, in_=banks[bk][:]), reads=[r_bank[bk]])
                else:
                    P.op("dve", lambda e, dst=dst, bk=bk: e.tensor_copy(out=dst, in_=banks[bk][:]), reads=[r_bank[bk]])
                ast['pcount'] += 1
                if b + 1 < 8:
                    if fc % 3 == 0:
                        a_norm_tile(b + 1, fc // 3)
                    elif fc % 3 == 2:
                        a_tr_tile(b + 1, fc // 3)
        P.fence()
        if "qkvT" in dumps:
            for fc in range(12):
                dump_sbuf("qkvT", qkvT[:, fc, :], row0=fc)
        if stop_after == "A":
            return _finish(nc, P, fin_res)

        A.reset(32 * 1024)
        Vlay = [A.alloc([32, 128], BF16) for _ in range(2)]
        acc = A.alloc([S], F32)
        assert A.off <= off_after_cat
        A.reset(off_after_qkv)
        Pt = [A.alloc([256], BF16) for _ in range(4)]
        rden = A.alloc([2, 512], F32)
        rsh = A.alloc([2, 512], F32)
        sqb = A.alloc([4, 512], BF16)
        rsb = A.alloc([512], F32)
        rstdb = A.alloc([512], F32)
        r_V = [P.res("V0"), P.res("V1")]
        Pt.append(rsb.bitcast(BF16)[:, 0:256])
        Pt.append(rstdb.bitcast(BF16)[:, 0:256])
        r_Pt = [P.res() for _ in range(6)]
        r_accpat = P.res("accpat")
        r_rden = [P.res(), P.res()]
        r_rsh = [P.res(), P.res()]
        r_accn = [P.res() for _ in range(8)]
        tmpb = A.alloc([2, 512], BF16)
        r_tmpb = [P.res(), P.res()]
        kz = A.alloc([S], BF16)
        r_kz = [P.res() for _ in range(8)]
        P.op("pool", lambda e: e.memset(Vlay[0][:, :, 64:128], 1.0), writes=[r_V[0]])
        P.op("pool", lambda e: e.memset(Vlay[1][:, :, 64:128], 1.0), writes=[r_V[1]])
        BK_S = (0, 1, 2, 3, 4)
        BK_O = (5, 6, 7)
        s_cnt = [0]
        NH = int(os.environ.get('K_BHEADS', 8))
        tasks = []
        for h in range(NH):
            for p, d in enumerate(PATTERNS):
                nb = 32 // d
                for r_ in range(d):
                    for n_ in range(nb):
                        tasks.append((h, p, d, nb, r_, n_))
        LA = 4
        g_state = [0]

        VBLA = 8

        def build_V(i):
            h, p, d, nb, r_, n_ = tasks[i]
            hp, hs = h // 2, (h % 2) * 64
            vb = (h * 3 + p) % 2
            L, r_L = Vlay[vb], r_V[vb]
            vrow = qkvT[hs:hs + 64, 8 + hp, :]
            if True:
                idh = ident[hs:hs + 64, hs:hs + 64]
                for half in range(2):
                    BK_V = BK_S[s_cnt[0] % 5]
                    s_cnt[0] += 1
                    pv = bank_bf(BK_V).rearrange("p (a b) -> p a b", b=64)
                    def trv(e, half=half, pv=pv):
                        ins = None
                        for ii in range(16):
                            s_ = half * 16 + ii
                            rr_, nn_ = s_ // nb, s_ % nb
                            ins = e.transpose(pv[:, ii, :], in_=vrow[:, _sl(128 * d * nn_ + rr_, 128, d)], identity=idh)
                        return ins
                    P.op("pe", trv, writes=[r_bank[BK_V]])
                    dst = L[:, half * 16:(half + 1) * 16, 0:64]
                    if half == 0:
                        P.op("act", lambda e, dst=dst, pv=pv: e.copy(out=dst, in_=pv), reads=[r_bank[BK_V]], writes=[r_L])
                    else:
                        P.op("dve", lambda e, dst=dst, pv=pv: e.tensor_copy(out=dst, in_=pv), reads=[r_bank[BK_V]], writes=[r_L])

        def att_stage1(i):
            if i == 0:
                build_V(0)
            j = i + VBLA
            if j < len(tasks) and tasks[j][4] == 0 and tasks[j][5] == 0:
                build_V(j)
            h, p, d, nb, r_, n_ = tasks[i]
            hp, hs = h // 2, (h % 2) * 64
            if p == 0 and n_ < 8:
                c = n_
                cs = slice(c * 512, (c + 1) * 512)
                P.op("dve", lambda e: e.tensor_copy(out=kz[hs:hs + 64, cs], in_=qkvT[hs:hs + 64, 4 + hp, cs]), writes=[r_kz[c]])
                P.op("pool", lambda e: e.memset(kz[64 - hs:128 - hs, cs], 0.0), writes=[r_kz[c]])
            qrow = qkvT[:, hp, :]
            krow = kz
            base = 128 * d * n_ + r_
            ncol = 256 if n_ + 1 < nb else 128
            bs = BK_S[s_cnt[0] % 5]
            s_cnt[0] += 1
            pt, r_pt = Pt[i % 6], r_Pt[i % 6]
            ksl = krow[:, _sl(base, 128, d)]
            qsl = qrow[:, _sl(base, ncol, d)]
            mk = mask[:, p * 8 + h, :]
            P.op("pe", lambda e: e.matmul(banks[bs][:, 0:ncol], lhsT=ksl, rhs=qsl, start=True, stop=True),
                 reads=r_kz[base // 512:(base + 127 * d) // 512 + 1], writes=[r_bank[bs]])
            P.op("act", lambda e: e.activation(out=pt[:, 0:ncol], in_=banks[bs][:, 0:ncol], func=AF.Exp, scale=0.125),
                 reads=[r_bank[bs]], writes=[r_pt])
            P.op("dve" if i % 2 == 1 else "pool",
                 lambda e: e.tensor_tensor(out=pt[:, 0:ncol], in0=pt[:, 0:ncol], in1=mk[:, 0:ncol], op=ALU.mult),
                 reads=[r_pt], writes=[r_pt])

        def att_stage2(i):
            h, p, d, nb, r_, n_ = tasks[i]
            if p == 0 and pending_norm and n_ % 4 == 0:
                while pending_norm and pending_norm[0][1] <= n_ // 4:
                    emit_norm()
            hp, hs = h // 2, (h % 2) * 64
            vb = (h * 3 + p) % 2
            L, r_L = Vlay[vb], r_V[vb]
            base = 128 * d * n_ + r_
            pt, r_pt = Pt[i % 6], r_Pt[i % 6]
            slot = r_ * nb + n_
            g = g_state[0]
            g_state[0] += 1
            bo = BK_O[g % 3]
            P.op("pe", lambda e: e.matmul(banks[bo][:, 0:128], lhsT=L[:, slot, :], rhs=pt[:, 0:128],
                                          start=(n_ == 0), stop=True, skip_group_check=True),
                 reads=[r_pt, r_L], writes=[r_bank[bo]])
            if n_ + 1 < nb:
                bo2 = BK_O[(g + 1) % 3]
                P.op("pe", lambda e: e.matmul(banks[bo2][:, 0:128], lhsT=L[:, slot, :], rhs=pt[:, 128:256],
                                              start=True, stop=False, skip_group_check=True),
                     reads=[r_pt, r_L], writes=[r_bank[bo2]])
            av = acc[:, _sl(base, 128, d)]
            first_acc = (r_ == 0 and n_ == 0)
            if p == 0:
                rw = dict(reads=[r_bank[bo], r_accpat], writes=[r_accn[base // 512]])
            elif first_acc:
                rw = dict(reads=[r_bank[bo]], writes=[r_accpat])
            else:
                rw = dict(reads=[r_bank[bo], r_accpat])
            if p == 0:
                P.op("dve", lambda e: e.tensor_copy(out=av, in_=banks[bo][:, 0:128]), **rw)
            else:
                P.op("dve", lambda e: e.tensor_tensor(out=av, in0=banks[bo][:, 0:128], in1=av, op=ALU.add), **rw)
            last_of_head = (p == len(PATTERNS) - 1 and r_ == d - 1 and n_ == nb - 1)
            if last_of_head:
                for tb in range(8):
                    pending_norm.append((h, tb))
                emit_norm()

        pending_norm = []

        def emit_norm():
            h, tb = pending_norm.pop(0)
            hp, hs = h // 2, (h % 2) * 64
            cs = slice(tb * 512, (tb + 1) * 512)
            u = tb % 2
            rw = dict(reads=[r_accn[tb]], writes=[r_accpat, r_rden[u]]) if tb == 0 else dict(reads=[r_accpat, r_accn[tb]], writes=[r_rden[u]])
            P.op("act", lambda e: e.activation(out=rden[64:128, u, :], in_=acc[64:128, cs], func=AF.Ln), **rw)
            P.op("act", lambda e: e.activation(out=rsh[0:64, u, :], in_=rden[64:128, u, :], func=AF.Exp, scale=-1.0),
                 reads=[r_rden[u]], writes=[r_rsh[u]])
            eng = "pool" if tb % 2 == 0 else "dve"
            if hs == 0:
                P.op(eng, lambda e: e.tensor_tensor(out=catT[0:64, hp, cs], in0=acc[0:64, cs], in1=rsh[0:64, u, :], op=ALU.mult),
                     reads=[r_rsh[u], r_accpat, r_accn[tb]], writes=[])
            else:
                P.op(eng, lambda e: e.tensor_tensor(out=tmpb[0:64, u, :], in0=acc[0:64, cs], in1=rsh[0:64, u, :], op=ALU.mult),
                     reads=[r_rsh[u], r_accpat, r_accn[tb]], writes=[r_tmpb[u]])
                P.op("act", lambda e: e.copy(out=catT[64:128, hp, cs], in_=tmpb[0:64, u, :]), reads=[r_tmpb[u]], writes=[])

        for i in range(len(tasks) + LA):
            if i < len(tasks):
                att_stage1(i)
            if i - LA >= 0:
                att_stage2(i - LA)
        while pending_norm:
            emit_norm()
        P.fence()
        if stop_after == "B0":
            if "catT" in dumps:
                for c in range(8):
                    dump_sbuf("catT", catT[:, c, :], row0=c)
            return _finish(nc, P, fin_res)
        _save_off = A.off
        A.reset(off_after_cat)
        w_h = A.alloc([8, 2048], BF16)
        A.reset(_save_off)
        r_wh = [P.res() for _ in range(4)]
        for i in (2, 0, 3, 1):
            P.dma("pool", w_h[:, :, i * 512:(i + 1) * 512], w_in_v[:, :, 1536 + i * 512:1536 + (i + 1) * 512], writes=[r_wh[i]])
        if stop_after == "B":
            if "catT" in dumps:
                for c in range(8):
                    dump_sbuf("catT", catT[:, c, :], row0=c)
            return _finish(nc, P, fin_res)

        A.reset(off_after_cat)
        w_h = A.alloc([8, 2048], BF16)
        g1bc = A.alloc([D], F32)
        hT = [A.alloc([8, 512], BF16) for _ in range(2)]
        xin = [A.alloc([D], F32) for _ in range(2)]
        xs = [A.alloc([D], BF16) for _ in range(2)]
        junk = A.alloc([D], BF16)
        Vtok = [A.alloc([4, 512], BF16) for _ in range(2)]
        zeros = A.alloc([512], F32)

        def hbuf(dt, n=2):
            bl = [A.alloc([512], dt) for _ in range(n)]
            return bl if n == 2 else [bl[0], bl[0]]
        qs_b, Pc_b, sg_b = [hbuf(F32) for _ in range(3)]
        f_b, kk_b, R_b = [hbuf(F32, 1) for _ in range(3)]
        Qd_b, Kd_b, Kd2_b = [hbuf(BF16) for _ in range(3)]
        gs_b = [A.alloc([512], BF16) for _ in range(3)]
        r_gs3 = [P.res() for _ in range(3)]
        Kd2tok_b = [A.alloc([4, 128], BF16) for _ in range(2)]
        Am_b = [A.alloc([4, 128], BF16) for _ in range(2)]
        sqh_b = hbuf(BF16)
        rsh_b = hbuf(F32)
        y_b = hbuf(F32)
        r_g1 = P.res("g1")
        r_hT = [[P.res() for _ in range(4)] for _ in range(2)]
        r_xin = [P.res() for _ in range(2)]
        r_xs = [P.res() for _ in range(2)]
        r_Vtok = [[P.res() for _ in range(4)] for _ in range(2)]
        r_S = [[P.res(), P.res()] for hd in range(4)]
        r_Sbf = [[P.res() for _ in range(9)] for _ in range(4)]
        r_lb = P.res("lb")
        r_zero = P.res("zero")

        def rr(n):
            return [[P.res() for _ in range(2)] for _ in range(n)]
        def share(r):
            r[0][1] = r[0][0]
        (r_qs, r_sg, r_f, r_kk, r_Pc, r_R, r_E, r_Qd, r_Kd, r_Kd2, r_gs, r_Kd2tok, r_Am, r_sqh, r_rsh2, r_y) = [[x] for x in rr(16)]
        for r_ in (r_f, r_kk, r_R):
            share(r_)

        P.dma("sp", g1bc, g1_d, writes=[r_g1])
        P.op("pool", lambda e: e.memset(zeros, 0.0), writes=[r_zero])
        P.op("pool", lambda e: e.memset(Sst[:].rearrange("p a b c -> p (a b c)"), 0.0), writes=[r_S[hd][0] for hd in range(4)])
        P.op("pool", lambda e: e.memset(pshare[:, 0:4 * 9 * 128], 0.0), writes=[r_Sbf[hd][0] for hd in range(4)])
        lv = lbl[:].rearrange("p (h l) -> p h l", l=2)
        P.op("dve", lambda e: e.tensor_tensor(out=lbv[:, 12:16], in0=lv[:, :, 0], in1=lv[:, :, 1], op=ALU.subtract), writes=[r_lb])
        P.op("act", lambda e: e.activation(out=lbv[:, 0:4], in_=lbv[:, 12:16], func=AF.Sigmoid), reads=[r_lb], writes=[r_lb])
        P.op("act", lambda e: e.activation(out=lbv[:, 4:8], in_=lbv[:, 12:16], func=AF.Sigmoid, scale=-1.0), reads=[r_lb], writes=[r_lb])
        P.op("dve", lambda e: e.tensor_scalar(out=lbv[:, 8:12], in0=lbv[:, 4:8], scalar1=-1.0, scalar2=0.0, op0=ALU.mult, op1=ALU.add),
             reads=[r_lb], writes=[r_lb])
        BK_T, BK_P, BK_A, BK_U, BK_O2 = 0, (1, 2), 3, (5, 6), (4, 7)
        NBLK = int(os.environ.get('K_CBLOCKS', 8))
        cst = dict(tcount=0, pc=0)

        def bp_norm(b, j):
            t = b * 4 + j
            sb = j % 2
            norm_tile(x_d[t * 128:(t + 1) * 128, :], g1bc, xin[sb], r_xin[sb], xs[sb], r_xs[sb], sb, junk, r_g1)

        def bp_tr(b, j):
            sb = j % 2
            transpose_tile(xs[sb], r_xs[sb], BK_T, hT[b % 2][:, :, j * 128:(j + 1) * 128], r_hT[b % 2][j],
                           "act" if j % 2 == 0 else "dve")

        def bp_vtok(b):
            hb = b % 2
            for j in range(4):
                bk = BK_P[cst['pc'] % 2]
                cst['pc'] += 1
                pairs = [(hT[hb][:, kc, j * 128:(j + 1) * 128], w_h[:, kc, 1024:1536]) for kc in range(8)]
                proj_group(bk, 512, pairs, reads=[r_wh[2], r_hT[hb][j]])
                P.op("dve", lambda e, bk=bk, j=j: e.tensor_copy(out=Vtok[hb][:, j, :], in_=banks[bk][:]),
                     reads=[r_bank[bk]], writes=[r_Vtok[hb][j]])

        def block_prep(b, stage=None):
            if stage is None:
                for j in range(4):
                    bp_norm(b, j)
                    bp_tr(b, j)
                bp_vtok(b)
            elif stage == 0:
                bp_norm(b, 0)
                bp_norm(b, 1)
            elif stage == 1:
                bp_tr(b, 0)
                bp_tr(b, 1)
                bp_norm(b, 2)
                bp_norm(b, 3)
            else:
                bp_tr(b, 2)
                bp_tr(b, 3)
                bp_vtok(b)

        def hg_A(k):
            b, hd = divmod(k, 4)
            hb, w = b % 2, k % 2
            qs, sg, gs = qs_b[w], sg_b[w], gs_b[k % 3]
            for col0, func, dst, r_dst in ((0, AF.Silu, qs, r_qs[0][w]), (1536, AF.Silu, gs, r_gs3[k % 3]), (512, AF.Sigmoid, sg, r_sg[0][w])):
                bk = BK_P[cst['pc'] % 2]
                cst['pc'] += 1
                c0 = col0 + hd * 128
                pairs = [(w_h[:, kc, c0:c0 + 128], hT[hb][:, kc, :]) for kc in range(8)]
                proj_group(bk, 512, pairs, reads=[r_wh[col0 // 512]] + r_hT[hb])
                P.op("act", lambda e, bk=bk, dst=dst, func=func: e.activation(out=dst, in_=banks[bk][:], func=func),
                     reads=[r_bank[bk]], writes=[r_dst])

        def hg_B(k):
            b, hd = divmod(k, 4)
            hb, w = b % 2, k % 2
            qs, sg, f_, kk, Pc, R_ = qs_b[w], sg_b[w], f_b[w], kk_b[w], Pc_b[w], R_b[w]
            Qd, Kd, Kd2 = Qd_b[w], Kd_b[w], Kd2_b[w]
            Kd2tok, Am = Kd2tok_b[w], Am_b[w]
            P.op("dve", lambda e: e.tensor_scalar(out=f_, in0=sg, scalar1=lbv[:, 4 + hd:5 + hd], scalar2=lbv[:, hd:hd + 1],
                                                  op0=ALU.mult, op1=ALU.add),
                 reads=[r_sg[0][w], r_lb], writes=[r_f[0][w]])
            P.op("pool", lambda e: e.tensor_scalar(out=kk, in0=sg, scalar1=lbv[:, 8 + hd:9 + hd], scalar2=lbv[:, 4 + hd:5 + hd],
                                                   op0=ALU.mult, op1=ALU.add),
                 reads=[r_sg[0][w], r_lb], writes=[r_kk[0][w]])

            def scan(e):
                ins = None
                for c in range(8):
                    cs = slice(c * 64, (c + 1) * 64)
                    ins = e.tensor_tensor_scan(out=Pc[:, cs], data0=f_[:, cs], data1=zeros[:, cs], initial=1.0,
                                               op0=ALU.mult, op1=ALU.add)
                return ins
            P.op("dve", scan, reads=[r_f[0][w], r_zero], writes=[r_Pc[0][w]])
            P.op("dve", lambda e: e.reciprocal(out=R_, in_=Pc), reads=[r_Pc[0][w]], writes=[r_R[0][w]])
            Pc3 = Pc.rearrange("p (c j) -> p c j", j=64)
            P.op("pool", lambda e: e.tensor_tensor(out=Qd, in0=qs, in1=Pc, op=ALU.mult),
                 reads=[r_qs[0][w], r_Pc[0][w]], writes=[r_Qd[0][w]])
            P.op("dve", lambda e: e.tensor_tensor(out=Kd, in0=kk, in1=R_, op=ALU.mult),
                 reads=[r_kk[0][w], r_R[0][w]], writes=[r_Kd[0][w]])
            P.op("pool", lambda e: e.tensor_tensor(out=Kd2.rearrange("p (c j) -> p c j", j=64), in0=Kd.rearrange("p (c j) -> p c j", j=64),
                                                   in1=Pc3[:, :, 63:64].to_broadcast([128, 8, 64]), op=ALU.mult),
                 reads=[r_Kd[0][w], r_Pc[0][w]], writes=[r_Kd2[0][w]])
            if k >= 1:
                hg_C2(k - 1)
            pt2 = bank_bf(BK_T).rearrange("p (a b) -> p a b", b=128)

            def trk(e):
                ins = None
                for j in range(4):
                    ins = e.transpose(pt2[:, j, :], in_=Kd2[:, j * 128:(j + 1) * 128], identity=ident[:])
                return ins
            P.op("pe", trk, reads=[r_Kd2[0][w]], writes=[r_bank[BK_T]])
            P.op("act", lambda e: e.copy(out=Kd2tok, in_=pt2[:, 0:4, :]), reads=[r_bank[BK_T]], writes=[r_Kd2tok[0][w]])
            pa = banks[BK_A][:].rearrange("p (a b) -> p a b", b=128)

            def mma(e):
                ins = None
                for j in range(4):
                    js = slice(j * 128, (j + 1) * 128)
                    ins = e.matmul(pa[:, j, :], lhsT=Kd[:, js], rhs=Qd[:, js], start=True, stop=True)
                return ins
            P.op("pe", mma, reads=[r_Kd[0][w], r_Qd[0][w]], writes=[r_bank[BK_A]])
            P.op("dve", lambda e: e.tensor_tensor(out=Am, in0=pa, in1=mbd[:].unsqueeze(1).to_broadcast([128, 4, 128]), op=ALU.mult),
                 reads=[r_bank[BK_A]], writes=[r_Am[0][w]])
            for hf in range(2):
                bu = BK_U[hf]
                pu = banks[bu][:].rearrange("p (a b) -> p a b", b=128)

                def mmu(e, hf=hf, pu=pu):
                    ins = None
                    ps_ = slice(hf * 64, hf * 64 + 64)
                    for j in range(4):
                        ins = e.matmul(pu[:, j, :], lhsT=Kd2tok[ps_, j, :], rhs=Vtok[hb][ps_, j, hd * 128:(hd + 1) * 128],
                                       start=True, stop=True)
                    return ins
                P.op("pe", mmu, reads=[r_Kd2tok[0][w]] + r_Vtok[hb], writes=[r_bank[bu]])
            bo = BK_O2[w]
            po = banks[bo]

            def mmi(e):
                ins = None
                for j in range(4):
                    ins = e.matmul(po[:, j * 128:(j + 1) * 128], lhsT=Vtok[hb][:, j, hd * 128:(hd + 1) * 128], rhs=Am[:, j, :],
                                   start=(j == 0), stop=False, skip_group_check=True)
                return ins
            P.op("pe", mmi, reads=[r_Am[0][w]] + r_Vtok[hb], writes=[r_bank[bo]])
            for c in range(8):
                m = b * 8 + c
                bu = BK_U[c % 2]
                pu = banks[bu][:].rearrange("p (a b) -> p a b", b=128)
                src, dst = Sst[:, hd, m % 2, :], Sst[:, hd, (m + 1) % 2, :]
                P.op("dve", lambda e, c=c, pu=pu, src=src, dst=dst:
                     e.scalar_tensor_tensor(out=dst, in0=src, scalar=Pc[:, c * 64 + 63:c * 64 + 64],
                                            in1=pu[:, c // 2, :], op0=ALU.mult, op1=ALU.add),
                     reads=[r_bank[bu], r_Pc[0][w], r_S[hd][m % 2]], writes=[r_S[hd][(m + 1) % 2]])
                P.op("act", lambda e, m=m, dst=dst: e.copy(out=Sbf[:, hd, (m + 1) % 9, :], in_=dst),
                     reads=[r_S[hd][(m + 1) % 2]], writes=[r_Sbf[hd][(m + 1) % 9]])

        def hg_C1(k):
            b, hd = divmod(k, 4)
            w = k % 2
            Qd = Qd_b[w]
            sqh, rs2 = sqh_b[w], rsh_b[w]
            bo = BK_O2[w]
            po = banks[bo]

            def mmx(e):
                ins = None
                for c in range(8):
                    ins = e.matmul(po[:, c * 64:(c + 1) * 64], lhsT=Sbf[:, hd, (b * 8 + c) % 9, :], rhs=Qd[:, c * 64:(c + 1) * 64],
                                   start=False, stop=(c == 7), skip_group_check=True)
                return ins
            P.op("pe", mmx, reads=[r_Sbf[hd][(b * 8 + c) % 9] for c in range(8)] + [r_Qd[0][w]], writes=[r_bank[bo]])
            P.op("act", lambda e: e.activation(out=sqh, in_=po[:], func=AF.Square), reads=[r_bank[bo]], writes=[r_sqh[0][w]])
            bk = BK_P[cst['pc'] % 2]
            cst['pc'] += 1
            proj_group(bk, 512, [(ones_bf[:], sqh)], reads=[r_sqh[0][w]])
            P.op("act", lambda e: e.activation(out=rs2, in_=banks[bk][:], func=AF.Ln, bias=EPS, scale=1.0 / 128),
                 reads=[r_bank[bk]], writes=[r_rsh2[0][w]])
            P.op("act", lambda e: e.activation(out=rs2, in_=rs2, func=AF.Exp, scale=-0.5),
                 reads=[r_rsh2[0][w]], writes=[r_rsh2[0][w]])

        def hg_C2(k):
            b, hd = divmod(k, 4)
            w = k % 2
            bs_ = slice(b * 512, (b + 1) * 512)
            gs = gs_b[k % 3]
            rs2, y_ = rsh_b[w], y_b[w]
            bo = BK_O2[w]
            po = banks[bo]
            P.op("dve", lambda e: e.tensor_tensor(out=y_, in0=po[:], in1=rs2, op=ALU.mult),
                 reads=[r_bank[bo], r_rsh2[0][w]], writes=[r_y[0][w]])
            P.op("dve", lambda e: e.scalar_tensor_tensor(out=catT[:, 4 + hd, bs_], in0=y_, scalar=gh[:, hd:hd + 1], in1=gs,
                                                         op0=ALU.mult, op1=ALU.mult),
                 reads=[r_y[0][w], r_gs3[k % 3]], writes=[])

        block_prep(0)
        NK = NBLK * 4
        hg_A(0)
        for k in range(NK):
            b, hd = divmod(k, 4)
            if hd < 3 and b + 1 < NBLK:
                block_prep(b + 1, stage=hd)
            if k + 1 < NK:
                hg_A(k + 1)
            if k >= 1:
                hg_C1(k - 1)
            hg_B(k)
        hg_C1(NK - 1)
        hg_C2(NK - 1)
        P.muted = False
        P.fence()
        if "catT" in dumps:
            for c in range(8):
                dump_sbuf("catT", catT[:, c, :], row0=c)
        if stop_after == "C":
            return _finish(nc, P, fin_res)

        KB = 1024
        A.reset(off_after_cat)
        w_o = A.alloc([8, D], BF16)
        gpbc = A.alloc([D], F32)
        xin = [A.alloc([D], F32) for _ in range(2)]
        mix = [A.alloc([D], F32) for _ in range(2)]
        junk = A.alloc([D], BF16)
        assert A.off <= 104 * KB
        A.reset(104 * KB)
        w1 = A.alloc([8, DFF], BF16)
        xin += [A.alloc([D], F32) for _ in range(1)]
        mix += [A.alloc([D], F32) for _ in range(1)]
        NDB = 3
        sqb = A.alloc([4, 512], BF16)
        rsb = A.alloc([512], F32)
        rstdb = A.alloc([512], F32)
        r_sqb = [P.res() for _ in range(4)]
        r_rsb = P.res()
        r_rstdb = P.res()

        def bnorm_block(b):
            bs_ = slice(b * 512, (b + 1) * 512)
            for c in range(4):
                if c % 2 == 0:
                    P.op("act", lambda e, c=c: e.activation(out=sqb[:, c, :], in_=catT[:, c, bs_], func=AF.Square),
                         reads=[r_cat[c][b]], writes=[r_sqb[c]])
                else:
                    P.op("dve", lambda e, c=c: e.tensor_tensor(out=sqb[:, c, :], in0=catT[:, c, bs_], in1=catT[:, c, bs_], op=ALU.mult),
                         reads=[r_cat[c][b]], writes=[r_sqb[c]])
            bk = 6 + b % 2
            proj_group(bk, 512, [(ones_bf[:], sqb[:, c, :]) for c in range(4)], reads=r_sqb)
            P.op("act", lambda e: e.activation(out=rsb, in_=banks[bk][:], func=AF.Ln, bias=EPS, scale=1.0 / 512),
                 reads=[r_bank[bk]], writes=[r_rsb])
            P.op("act", lambda e: e.activation(out=rstdb, in_=rsb, func=AF.Exp, scale=-0.5), reads=[r_rsb], writes=[r_rstdb])
            for c in range(4):
                P.op("dve", lambda e, c=c: e.scalar_tensor_tensor(out=catT[:, c, bs_], in0=catT[:, c, bs_], scalar=ga[:, c:c + 1],
                                                                 in1=rstdb, op0=ALU.mult, op1=ALU.mult),
                     reads=[r_rstdb], writes=[r_cat[c][b]])
        r_wo = [P.res(), P.res()]
        r_catD = P.res()
        r_x1s = [P.res() for _ in range(32)]
        r_w1 = [P.res() for _ in range(8)]
        r_gp = P.res("gp")
        r_xin = [P.res() for _ in range(NDB)]
        r_mix = [P.res() for _ in range(NDB)]
        w_out_v = w_out_d.rearrange("(kc p) f -> p kc f", p=128)
        w1_v = w1_d.rearrange("(kc p) f -> p kc f", p=128)
        w2_v = w2_d.rearrange("(kc p) f -> p kc f", p=128)
        P.dma("pool", w_o[:, :, 0:512], w_out_v[:, :, 0:512], writes=[r_wo[0]])
        P.dma("pool", w_o[:, :, 512:1024], w_out_v[:, :, 512:1024], writes=[r_wo[1]])
        P.dma("sp", gpbc, gpost_d, writes=[r_gp])
        bnorm_block(0)
        bnorm_block(1)
        for t in range(32):
            sb = t % NDB
            ts_ = slice(t * 128, (t + 1) * 128)
            if t % 4 == 1 and t // 4 + 2 < 8:
                bnorm_block(t // 4 + 2)
            if t % 3 == 1 and t // 3 < 8:
                i = t // 3
                P.dma("pool", w1[:, :, i * 512:(i + 1) * 512], w1_v[:, :, i * 512:(i + 1) * 512], writes=[r_w1[i]])
            P.dma("sp", xin[sb], x_d[ts_, :], writes=[r_xin[sb]])
            for half in range(2):
                bk = (t % 3) * 2 + half
                pairs = [(catT[:, kc, ts_], w_o[:, kc, half * 512:(half + 1) * 512]) for kc in range(8)]
                proj_group(bk, 512, pairs, reads=[r_wo[half], r_catD] + [r_cat[c][t // 4] for c in range(4)])
                P.op("act", lambda e, bk=bk, sb=sb, half=half: e.copy(out=mix[sb][:, half * 512:(half + 1) * 512], in_=banks[bk][:]),
                     reads=[r_bank[bk]], writes=[r_mix[sb]])
            _post_norm_residual(P, mix[sb], r_mix[sb], xin[sb], r_xin[sb], gpbc, r_gp, stat, r_stat[sb], sb, junk)
            P.dma("pool", x1s_d[ts_, :], mix[sb], reads=[r_mix[sb]], writes=[r_x1s[t]])
        w2 = arena_t[:, 0:(32 * D)].rearrange("p (a b) -> p a b", b=D)
        r_w2 = [P.res() for _ in range(8)]
        for i in range(8):
            P.dma("pool", w2[:, i * 4:(i + 1) * 4, :], w2_v[:, i * 4:(i + 1) * 4, :], writes=[r_w2[i], r_catD] if i == 0 else [r_w2[i]])
        P.fence(skip_dma_queues=("pool",))
        if stop_after == "D":
            return _finish(nc, P, fin_res)

        A.reset(64 * KB)
        g2bc = A.alloc([D], F32)
        gp2bc = A.alloc([D], F32)
        xin = [[A.alloc([D], F32) for _ in range(2)] for _ in range(2)]
        xs = [A.alloc([D], BF16) for _ in range(2)]
        h2T = [A.alloc([8, 256], BF16) for _ in range(2)]
        ffo = A.alloc([D], F32)
        assert A.off <= 104 * KB, A.off
        A.reset(168 * KB)
        uT = A.alloc([32, 256], BF16)
        rl2 = A.alloc([2, 512], BF16)
        rl = [rl2[:, 0, :], rl2[:, 1, :]]
        junk = A.alloc([D], BF16)
        r_g2 = P.res("g2")
        r_gp2 = P.res("gp2")
        r_xin = [[P.res() for _ in range(2)] for _ in range(2)]
        r_xs = [P.res() for _ in range(2)]
        r_h2T = [[P.res() for _ in range(2)] for _ in range(2)]
        r_uT = [P.res() for _ in range(16)]
        r_rl = [P.res() for _ in range(2)]
        r_ffo = P.res()
        P.dma("sp", g2bc, g2_d, writes=[r_g2])
        P.dma("sp", gp2bc, gpost2_d, writes=[r_gp2])
        BK_T = 0
        BK_F1 = (1, 2, 3)
        BK_F2 = (4, 5, 6, 7)
        NEB = 16

        def e_norm(b):
            for j in range(2):
                t = b * 2 + j
                norm_tile(x1s_d[t * 128:(t + 1) * 128, :], g2bc, xin[b % 2][j], r_xin[b % 2][j], xs[j], r_xs[j], j, junk, r_g2, src_reads=[r_x1s[t]])

        def e_tr(b):
            for j in range(2):
                transpose_tile(xs[j], r_xs[j], BK_T, h2T[b % 2][:, :, j * 128:(j + 1) * 128], r_h2T[b % 2][j],
                               "act" if j == 0 else "dve")

        est = dict(f1c=0)

        def e_ffn1(b, lo=0, hi=16):
            hb = b % 2
            for fp in range(lo, hi):
                bk = BK_F1[est['f1c'] % 3]
                w = est['f1c'] % 2
                est['f1c'] += 1
                pb = banks[bk][:].rearrange("p (a b) -> p a b", b=256)

                def mm1(e, pb=pb, fp=fp):
                    ins = None
                    for i in range(2):
                        fc = fp * 2 + i
                        for kc in range(8):
                            ins = e.matmul(pb[:, i, :], lhsT=w1[:, kc, fc * 128:(fc + 1) * 128], rhs=h2T[hb][:, kc, :],
                                           start=(kc == 0 and i == 0), stop=(kc == 7), skip_group_check=True)
                    return ins
                P.op("pe", mm1, reads=[r_w1[fp // 2]] + r_h2T[hb], writes=[r_bank[bk]])
                P.op("act", lambda e, w=w, bk=bk: e.activation(out=rl[w], in_=banks[bk][:], func=AF.Relu),
                     reads=[r_bank[bk]], writes=[r_rl[w]])
                P.op("dve" if (fp % 2 == 0 or b == 0) else "pool",
                     lambda e, w=w, fp=fp: e.tensor_tensor(out=uT[:, fp * 2:fp * 2 + 2, :], in0=rl[w].rearrange("p (a b) -> p a b", b=256),
                                                           in1=rl[w].rearrange("p (a b) -> p a b", b=256), op=ALU.mult),
                     reads=[r_rl[w]], writes=[r_uT[fp]])

        def e_ffn2(b):
            for j in range(2):
                t = b * 2 + j
                ts_ = slice(t * 128, (t + 1) * 128)
                xr, r_xr = xin[b % 2][j], r_xin[b % 2][j]
                for half in range(2):
                    bk = BK_F2[j * 2 + half]
                    for g8 in range(8):
                        def mm2(e, g8=g8, bk=bk, j=j, half=half):
                            ins = None
                            for fc in range(g8 * 4, g8 * 4 + 4):
                                ins = e.matmul(banks[bk][:, 0:512], lhsT=uT[:, fc, j * 128:(j + 1) * 128], rhs=w2[:, fc, half * 512:(half + 1) * 512],
                                               start=(fc == 0), stop=(fc == 31), skip_group_check=True)
                            return ins
                        P.op("pe", mm2, reads=[r_w2[g8], r_uT[g8 * 2], r_uT[g8 * 2 + 1]], writes=[r_bank[bk]])
                    P.op("act", lambda e, bk=bk, half=half: e.copy(out=ffo[:, half * 512:(half + 1) * 512], in_=banks[bk][:]),
                         reads=[r_bank[bk]], writes=[r_ffo])
                _post_norm_residual(P, ffo, r_ffo, xr, r_xr, gp2bc, r_gp2, stat, r_stat[2 + j], 2 + j, junk, into_res=True)
                ro = P.res()
                P.dma("pool", out_d[ts_, :], xr, reads=[r_xr], writes=[ro])
                fin_res.append(ro)

        e_norm(0)
        e_tr(0)
        for b in range(NEB):
            e_ffn1(b, 0, 8)
            if b + 1 < NEB:
                e_norm(b + 1)
            e_ffn1(b, 8, 16)
            if b + 1 < NEB:
                e_tr(b + 1)
            e_ffn2(b)
        return _finish(nc, P, fin_res)


def _post_norm_residual(P, buf, r_buf, xres, r_xres, gbc, r_g, stat, r_st, sb, junk, into_res=False):
    st0 = stat[:, 4 * sb:4 * sb + 1]
    st1 = stat[:, 4 * sb + 1:4 * sb + 2]
    st2 = stat[:, 4 * sb + 2:4 * sb + 3]
    P.op("act", lambda e: e.activation(out=junk, in_=buf, func=AF.Square, accum_out=st0), reads=[r_buf], writes=[r_st])
    P.op("act", lambda e: e.activation(out=st1, in_=st0, func=AF.Sqrt, bias=EPS, scale=1.0 / D), reads=[r_st], writes=[r_st])
    P.op("dve", lambda e: e.reciprocal(out=st2, in_=st1), reads=[r_st], writes=[r_st])
    P.op("dve", lambda e: e.scalar_tensor_tensor(out=buf, in0=buf, scalar=st2, in1=gbc, op0=ALU.mult, op1=ALU.mult),
         reads=[r_buf, r_st, r_g], writes=[r_buf])
    if into_res:
        P.op("pool", lambda e: e.tensor_tensor(out=xres, in0=buf, in1=xres, op=ALU.add), reads=[r_buf, r_xres], writes=[r_xres])
    else:
        P.op("pool", lambda e: e.tensor_tensor(out=buf, in0=buf, in1=xres, op=ALU.add), reads=[r_buf, r_xres], writes=[r_buf])


def _finish(nc, P, fin_res):
    P.fence()
    P.build()
    return nc


def _make_in_maps(inputs):
    f32 = np.float32
    x = np.ascontiguousarray(np.asarray(inputs["x"], dtype=f32))
    ident, mask, mbd = _const_tables()

    def bc(v):
        return np.ascontiguousarray(np.broadcast_to(np.asarray(v, dtype=f32).reshape(1, D), (128, D)))

    def fm(v):
        return np.ascontiguousarray(np.asarray(v, dtype=f32).reshape(4, 128).T)
    lbl = np.asarray(inputs["hgrn_lb_logits"], dtype=f32)
    lbl_l = np.ascontiguousarray(lbl.reshape(2, 4, 128).transpose(2, 1, 0).reshape(128, 8))
    shared = {
        "w_in": np.ascontiguousarray(np.asarray(inputs["w_in"], dtype=f32)[0]),
        "w_out": np.ascontiguousarray(np.asarray(inputs["w_out"], dtype=f32)[0]),
        "w_ff1": np.ascontiguousarray(np.asarray(inputs["w_ff1"], dtype=f32)[0]),
        "w_ff2": np.ascontiguousarray(np.asarray(inputs["w_ff2"], dtype=f32)[0]),
        "g1": bc(inputs["mix_pre_norm"][0]),
        "gpost": bc(inputs["mix_post_norm"][0]),
        "g2": bc(inputs["mlp_pre_norm"][0]),
        "gpost2": bc(inputs["mlp_post_norm"][0]),
        "ga": fm(inputs["attn_out_norm"][0]),
        "gh": fm(inputs["hgrn_out_norm"][0]),
        "lbl": lbl_l,
        "c_ident": ident,
        "c_mask": mask,
        "c_mbd": mbd,
    }
    return [dict(shared, x=x[b]) for b in range(NCORE)]


def kernel(**inputs):
    in_maps = _make_in_maps(inputs)
    nc = build_program()
    res = run_bass_kernel_spmd(nc, in_maps, core_ids=list(range(NCORE)))
    out = np.stack([np.asarray(r["out"]) for r in res.results], axis=0)
    return out.astype(np.float32)
```
